# Optimizing a Trainium2 kernel written in Bass

```python
import math
import jax, jax.numpy as jnp
from jax import lax
import numpy as np

D_MODEL = 1024
BATCH = 8
SEQ = 2048
DEPTH = 2

SSD_HEADS = 12
SSD_HEAD_DIM = 64
SSD_INNER = SSD_HEADS * SSD_HEAD_DIM
SSD_GROUPS = 4
SSD_HPG = SSD_HEADS // SSD_GROUPS
SSD_STATE = 64
SSD_CONV = 4
SSD_CHUNK = 128
SSD_CONV_CH = SSD_INNER + 2 * SSD_GROUPS * SSD_STATE

S5_GROUP = 16
S5_WIDTH = 512
S5_GROUPS = S5_WIDTH // S5_GROUP
S5_STATE = 64

SC_WIDTH = 512
SC_CONV = 3

CF_WIDTH = 512
CF_CONV = 31

N_BRANCH = 4
BRANCH_OFFSETS = (0, SSD_INNER, SSD_INNER + S5_WIDTH, SSD_INNER + S5_WIDTH + SC_WIDTH,
                  SSD_INNER + S5_WIDTH + SC_WIDTH + CF_WIDTH)
MIX_WIDTH = BRANCH_OFFSETS[-1]

OFF_Z = 0
OFF_XBC = OFF_Z + SSD_INNER
OFF_DT = OFF_XBC + SSD_CONV_CH
OFF_S5 = OFF_DT + SSD_HEADS
OFF_SC = OFF_S5 + S5_WIDTH
OFF_CF = OFF_SC + 3 * SC_WIDTH
OFF_GATE = OFF_CF + 2 * CF_WIDTH
IN_COLS = OFF_GATE + N_BRANCH * D_MODEL

PEER_HEADS = 8
PEER_KEYS = 128
PEER_TOPK = 16
PEER_QDIM = 256
PEER_HALF = PEER_QDIM // 2
N_EXPERTS = PEER_KEYS * PEER_KEYS
PEER_CHUNK = 128

EPS = 1e-6

kernel_name = "hybrid_ssd_s5_conv_peer_block"


def rms_norm(x, g):
    xf = x.astype(jnp.float32)
    y = xf * lax.rsqrt(jnp.mean(xf * xf, axis=-1, keepdims=True) + EPS)
    return (y * g.astype(jnp.float32)).astype(x.dtype)


def layer_norm(x, g, b):
    xf = x.astype(jnp.float32)
    mu = jnp.mean(xf, axis=-1, keepdims=True)
    var = jnp.mean(jnp.square(xf - mu), axis=-1, keepdims=True)
    y = (xf - mu) * lax.rsqrt(var + EPS)
    return (y * g.astype(jnp.float32) + b.astype(jnp.float32)).astype(x.dtype)


def causal_dwconv(x, w):
    k, c = w.shape
    return lax.conv_general_dilated(
        x, w[:, None, :].astype(x.dtype), window_strides=(1,), padding=[(k - 1, 0)],
        dimension_numbers=("NWC", "WIO", "NWC"), feature_group_count=c)


def segsum(a):
    t = a.shape[-1]
    rep = jnp.broadcast_to(a[..., :, None], a.shape + (t,))
    rep = jnp.where(jnp.tril(jnp.ones((t, t), bool), -1), rep, 0.0)
    cs = jnp.cumsum(rep, axis=-2)
    return jnp.where(jnp.tril(jnp.ones((t, t), bool), 0), cs, -jnp.inf)


def ssd_scan(xh, dt, a, bm, cm):
    bsz, s, g, j, p = xh.shape
    n = bm.shape[-1]
    nc, ln = s // SSD_CHUNK, SSD_CHUNK
    xdt = (xh * dt[..., None]).reshape(bsz, nc, ln, g, j, p)
    adt = (dt * a).reshape(bsz, nc, ln, g, j).transpose(0, 3, 4, 1, 2)
    bc = bm.reshape(bsz, nc, ln, g, n)
    cc = cm.reshape(bsz, nc, ln, g, n)
    a_cum = jnp.cumsum(adt, axis=-1)
    decay_in = jnp.exp(segsum(adt))
    cb = jnp.einsum("bclgn,bcsgn->bcgls", cc, bc)
    y_diag = jnp.einsum("bcgls,bgjcls,bcsgjp->bclgjp", cb, decay_in, xdt)
    decay_states = jnp.exp(a_cum[..., -1:] - a_cum)
    states = jnp.einsum("bclgn,bgjcl,bclgjp->bcgjpn", bc, decay_states, xdt)
    states = jnp.concatenate([jnp.zeros_like(states[:, :1]), states], axis=1)
    chunk_tot = jnp.pad(a_cum[..., -1], ((0, 0), (0, 0), (0, 0), (1, 0)))
    decay_chunk = jnp.exp(segsum(chunk_tot))
    states = jnp.einsum("bgjzc,bcgjpn->bzgjpn", decay_chunk, states)[:, :-1]
    y_off = jnp.einsum("bclgn,bcgjpn,bgjcl->bclgjp", cc, states, jnp.exp(a_cum))
    return (y_diag + y_off).reshape(bsz, s, g, j, p)


def ssd_branch(proj, conv_w, conv_b, dt_bias, a_log, d_skip, norm_g):
    f32 = jnp.float32
    bsz, s, _ = proj.shape
    z = proj[..., OFF_Z:OFF_Z + SSD_INNER]
    xbc = proj[..., OFF_XBC:OFF_XBC + SSD_CONV_CH]
    dt_raw = proj[..., OFF_DT:OFF_DT + SSD_HEADS]
    xbc = jax.nn.silu(causal_dwconv(xbc, conv_w) + conv_b.astype(proj.dtype)).astype(f32)
    gn = SSD_GROUPS * SSD_STATE
    xs = xbc[..., :SSD_INNER].reshape(bsz, s, SSD_GROUPS, SSD_HPG, SSD_HEAD_DIM)
    bm = xbc[..., SSD_INNER:SSD_INNER + gn].reshape(bsz, s, SSD_GROUPS, SSD_STATE)
    cm = xbc[..., SSD_INNER + gn:].reshape(bsz, s, SSD_GROUPS, SSD_STATE)
    dt = jax.nn.softplus(dt_raw.astype(f32) + dt_bias.astype(f32)).reshape(bsz, s, SSD_GROUPS, SSD_HPG)
    a = -jnp.exp(a_log.astype(f32)).reshape(SSD_GROUPS, SSD_HPG)
    y = ssd_scan(xs, dt, a, bm, cm) + d_skip.astype(f32).reshape(SSD_GROUPS, SSD_HPG)[..., None] * xs
    y = y.reshape(bsz, s, SSD_INNER) * jax.nn.silu(z.astype(f32))
    return rms_norm(y, norm_g).astype(proj.dtype)


def _lin_rec_combine(e_i, e_j):
    a_i, b_i = e_i
    a_j, b_j = e_j
    return a_j * a_i, a_j * b_i + b_j


def s5_branch(u, lam_re, lam_im, log_step, b_re, b_im, c_re, c_im, d_skip, w_glu):
    f32 = jnp.float32
    bsz, s, _ = u.shape
    lam = lax.complex(lam_re.astype(f32), lam_im.astype(f32))
    step = jnp.exp(log_step.astype(f32))[:, None]
    lam_bar = jnp.exp(lam * step)
    b_bar = ((lam_bar - 1.0) / lam)[..., None] * lax.complex(b_re.astype(f32), b_im.astype(f32))
    c = lax.complex(c_re.astype(f32), c_im.astype(f32))
    ug = u.astype(f32).reshape(bsz, s, S5_GROUPS, S5_GROUP)
    bu = jnp.einsum("bsgi,gni->bsgn", ug.astype(jnp.complex64), b_bar)
    a = jnp.broadcast_to(lam_bar, (1, s) + lam_bar.shape)
    _, hs = lax.associative_scan(_lin_rec_combine, (a, bu), axis=1)
    y = jnp.einsum("bsgn,gon->bsgo", hs, c).real + d_skip.astype(f32).reshape(S5_GROUPS, S5_GROUP) * ug
    y = jax.nn.gelu(y.reshape(bsz, s, S5_WIDTH)).astype(u.dtype)
    return y * jax.nn.sigmoid(y @ w_glu)


def shortconv_branch(p_sc, conv_w):
    b, c, h = jnp.split(p_sc, 3, axis=-1)
    return b * causal_dwconv(c * h, conv_w)


def conformer_branch(p_cf, conv_w, ln_g, ln_b):
    a, g = jnp.split(p_cf, 2, axis=-1)
    y = causal_dwconv(a * jax.nn.sigmoid(g), conv_w)
    return jax.nn.silu(layer_norm(y, ln_g, ln_b))


def peer_ffn(h, w_query, sub_keys, expert_u, expert_v):
    f32 = jnp.float32
    bsz, s, d = h.shape
    t = bsz * s
    ht = h.reshape(t, d)
    q = (ht @ w_query).reshape(t, PEER_HEADS, 2, PEER_HALF).astype(f32)
    sc = jnp.einsum("thpd,pkd->thpk", q, sub_keys.astype(f32))
    s_top, i_top = lax.top_k(sc, PEER_TOPK)
    cand = (s_top[:, :, 0, :, None] + s_top[:, :, 1, None, :]).reshape(t, PEER_HEADS, PEER_TOPK * PEER_TOPK)
    best, pos = lax.top_k(cand, PEER_TOPK)
    idx = (jnp.take_along_axis(i_top[:, :, 0, :], pos // PEER_TOPK, axis=-1) * PEER_KEYS
           + jnp.take_along_axis(i_top[:, :, 1, :], pos % PEER_TOPK, axis=-1))
    gate = jax.nn.softmax(best, axis=-1).astype(h.dtype)

    def chunk_fn(args):
        hc, ic, gc = args
        act = jax.nn.gelu(jnp.einsum("chkd,cd->chk", expert_u[ic], hc)) * gc
        return jnp.einsum("chk,chkd->cd", act, expert_v[ic])

    nc = t // PEER_CHUNK
    out = lax.map(chunk_fn, (ht.reshape(nc, PEER_CHUNK, d),
                             idx.reshape(nc, PEER_CHUNK, PEER_HEADS, PEER_TOPK),
                             gate.reshape(nc, PEER_CHUNK, PEER_HEADS, PEER_TOPK)))
    return out.reshape(bsz, s, d)


def setup_inputs(seed: int = 0) -> dict:
    key = jax.random.key(seed)
    ks = jax.random.split(key, 32)
    f32 = jnp.float32
    L = DEPTH

    def nrm(k, shape, scale):
        return jax.random.normal(k, shape, f32) * scale

    x = nrm(ks[0], (BATCH, SEQ, D_MODEL), 1.0)
    norm1_g = 1.0 + nrm(ks[1], (L, D_MODEL), 0.02)
    w_in = nrm(ks[2], (L, D_MODEL, IN_COLS), D_MODEL ** -0.5)
    ssd_conv_w = nrm(ks[3], (L, SSD_CONV, SSD_CONV_CH), SSD_CONV ** -0.5)
    ssd_conv_b = nrm(ks[4], (L, SSD_CONV_CH), 0.01)
    dt0 = jnp.exp(jax.random.uniform(ks[5], (L, SSD_HEADS), f32, math.log(1e-3), math.log(1e-1)))
    ssd_dt_bias = dt0 + jnp.log(-jnp.expm1(-dt0))
    ssd_a_log = jnp.log(jax.random.uniform(ks[6], (L, SSD_HEADS), f32, 1.0, 16.0))
    ssd_d = 1.0 + nrm(ks[7], (L, SSD_HEADS), 0.02)
    ssd_norm_g = 1.0 + nrm(ks[8], (L, SSD_INNER), 0.02)
    n_idx = jnp.arange(S5_STATE, dtype=f32)
    s5_lam_re = -0.5 + nrm(ks[9], (L, S5_GROUPS, S5_STATE), 0.01)
    s5_lam_im = math.pi * n_idx + nrm(ks[10], (L, S5_GROUPS, S5_STATE), 0.01)
    s5_log_step = jax.random.uniform(ks[11], (L, S5_GROUPS), f32, math.log(1e-3), math.log(1e-1))
    s5_b_re = nrm(ks[12], (L, S5_GROUPS, S5_STATE, S5_GROUP), (2 * S5_GROUP) ** -0.5)
    s5_b_im = nrm(ks[13], (L, S5_GROUPS, S5_STATE, S5_GROUP), (2 * S5_GROUP) ** -0.5)
    s5_c_re = nrm(ks[14], (L, S5_GROUPS, S5_GROUP, S5_STATE), (2 * S5_STATE) ** -0.5)
    s5_c_im = nrm(ks[15], (L, S5_GROUPS, S5_GROUP, S5_STATE), (2 * S5_STATE) ** -0.5)
    s5_d = nrm(ks[16], (L, S5_WIDTH), 1.0)
    s5_w_glu = nrm(ks[17], (L, S5_WIDTH, S5_WIDTH), S5_WIDTH ** -0.5)
    sc_conv_w = nrm(ks[18], (L, SC_CONV, SC_WIDTH), SC_CONV ** -0.5)
    cf_conv_w = nrm(ks[19], (L, CF_CONV, CF_WIDTH), CF_CONV ** -0.5)
    cf_ln_g = 1.0 + nrm(ks[20], (L, CF_WIDTH), 0.02)
    cf_ln_b = nrm(ks[21], (L, CF_WIDTH), 0.02)
    row_scale = jnp.concatenate([jnp.full((BRANCH_OFFSETS[i + 1] - BRANCH_OFFSETS[i],),
                                          float(BRANCH_OFFSETS[i + 1] - BRANCH_OFFSETS[i]) ** -0.5, f32)
                                 for i in range(N_BRANCH)])
    w_branch = nrm(ks[22], (L, MIX_WIDTH, D_MODEL), 1.0) * row_scale[None, :, None]
    w_out = nrm(ks[23], (L, D_MODEL, D_MODEL), D_MODEL ** -0.5)
    norm2_g = 1.0 + nrm(ks[24], (L, D_MODEL), 0.02)
    peer_w_query = nrm(ks[25], (L, D_MODEL, PEER_HEADS * PEER_QDIM), D_MODEL ** -0.5)
    peer_sub_keys = nrm(ks[26], (L, 2, PEER_KEYS, PEER_HALF), PEER_HALF ** -0.5)
    peer_u = nrm(ks[27], (L, N_EXPERTS, D_MODEL), D_MODEL ** -0.5)
    peer_v = nrm(ks[28], (L, N_EXPERTS, D_MODEL), (PEER_HEADS * PEER_TOPK) ** -0.5)
    final_norm_g = 1.0 + nrm(ks[29], (D_MODEL,), 0.02)
    return {
        "x": x, "norm1_g": norm1_g, "w_in": w_in,
        "ssd_conv_w": ssd_conv_w, "ssd_conv_b": ssd_conv_b, "ssd_dt_bias": ssd_dt_bias,
        "ssd_a_log": ssd_a_log, "ssd_d": ssd_d, "ssd_norm_g": ssd_norm_g,
        "s5_lam_re": s5_lam_re, "s5_lam_im": s5_lam_im, "s5_log_step": s5_log_step,
        "s5_b_re": s5_b_re, "s5_b_im": s5_b_im, "s5_c_re": s5_c_re, "s5_c_im": s5_c_im,
        "s5_d": s5_d, "s5_w_glu": s5_w_glu,
        "sc_conv_w": sc_conv_w,
        "cf_conv_w": cf_conv_w, "cf_ln_g": cf_ln_g, "cf_ln_b": cf_ln_b,
        "w_branch": w_branch, "w_out": w_out, "norm2_g": norm2_g,
        "peer_w_query": peer_w_query, "peer_sub_keys": peer_sub_keys,
        "peer_u": peer_u, "peer_v": peer_v, "final_norm_g": final_norm_g,
    }


def reference(x, norm1_g, w_in, ssd_conv_w, ssd_conv_b, ssd_dt_bias, ssd_a_log, ssd_d, ssd_norm_g,
              s5_lam_re, s5_lam_im, s5_log_step, s5_b_re, s5_b_im, s5_c_re, s5_c_im, s5_d, s5_w_glu,
              sc_conv_w, cf_conv_w, cf_ln_g, cf_ln_b, w_branch, w_out, norm2_g,
              peer_w_query, peer_sub_keys, peer_u, peer_v, final_norm_g):
    bsz, s, d = x.shape
    for l in range(DEPTH):
        h = rms_norm(x, norm1_g[l])
        proj = h @ w_in[l]
        y_a = ssd_branch(proj, ssd_conv_w[l], ssd_conv_b[l], ssd_dt_bias[l], ssd_a_log[l], ssd_d[l], ssd_norm_g[l])
        y_b = s5_branch(proj[..., OFF_S5:OFF_S5 + S5_WIDTH], s5_lam_re[l], s5_lam_im[l], s5_log_step[l],
                        s5_b_re[l], s5_b_im[l], s5_c_re[l], s5_c_im[l], s5_d[l], s5_w_glu[l])
        y_c = shortconv_branch(proj[..., OFF_SC:OFF_SC + 3 * SC_WIDTH], sc_conv_w[l])
        y_d = conformer_branch(proj[..., OFF_CF:OFF_CF + 2 * CF_WIDTH], cf_conv_w[l], cf_ln_g[l], cf_ln_b[l])
        ys = (y_a, y_b, y_c, y_d)
        gates = jax.nn.sigmoid(proj[..., OFF_GATE:]).reshape(bsz, s, N_BRANCH, d)
        merged = gates[:, :, 0, :] * (ys[0] @ w_branch[l, BRANCH_OFFSETS[0]:BRANCH_OFFSETS[1]])
        for i in range(1, N_BRANCH):
            merged = merged + gates[:, :, i, :] * (ys[i] @ w_branch[l, BRANCH_OFFSETS[i]:BRANCH_OFFSETS[i + 1]])
        x = x + merged @ w_out[l]
        x = x + peer_ffn(rms_norm(x, norm2_g[l]), peer_w_query[l], peer_sub_keys[l], peer_u[l], peer_v[l])
    return rms_norm(x, final_norm_g)
```

```python
import numpy as np
from contextlib import ExitStack
import concourse.bass as bass
import concourse.mybir as mybir
from concourse.bass_utils import run_bass_kernel_spmd

F32 = mybir.dt.float32
I32 = mybir.dt.int32
U16 = mybir.dt.uint16
ALU = mybir.AluOpType
AF = mybir.ActivationFunctionType
AX = mybir.AxisListType

L = 2
D = 1024
S = 2048
NT = S // 128
INC = 9228
EPS = 1e-6
TWO_PI = float(2 * np.pi)

ENGS = ("pe", "act", "dve", "pool", "sp")
SEM_LIMIT = 24000
DMA_SLOTS = 16
SAME_ENG_BIG = 1 << 60


class Prog:
    def __init__(self, nc):
        self.nc = nc
        self.es = ExitStack()
        self.streams = {e: [] for e in ENGS}
        self.nsem = 0
        self.esem = {e: self._newsem() for e in ENGS}
        self.ecnt = {e: 0 for e in ENGS}
        self.known = {e: {} for e in ENGS}
        self.state = {}
        self.children = {}
        self.dslots = {q: [[self._newsem(), 0] for _ in range(DMA_SLOTS)] for q in ("sp", "pool", "act")}
        self.dnext = {q: 0 for q in ("sp", "pool", "act")}
        self.nops = 0
        self.evsize = {}

    def _newsem(self):
        self.nsem += 1
        return self.es.enter_context(self.nc.semaphore("s%d" % self.nsem))

    @staticmethod
    def _norm(k):
        return k if isinstance(k, tuple) else (k,)

    def _related(self, k):
        out = []
        for i in range(1, len(k) + 1):
            p = k[:i]
            if p in self.state:
                out.append(p)
        for c in self.children.get(k, ()):
            if c != k:
                out.append(c)
        return out

    def _touch(self, k):
        if k not in self.state:
            self.state[k] = [None, {}]
            for i in range(1, len(k)):
                self.children.setdefault(k[:i], set()).add(k)
        return self.state[k]

    def _deps(self, reads, writes):
        deps = []
        for k in reads:
            for r in self._related(self._norm(k)):
                w = self.state[r][0]
                if w is not None:
                    deps.append(w)
        for k in writes:
            for r in self._related(self._norm(k)):
                st = self.state[r]
                if st[0] is not None:
                    deps.append(st[0])
                deps.extend(st[1].values())
        return deps

    def _record(self, eng, ev, reads, writes):
        for k in reads:
            st = self._touch(self._norm(k))
            st[1][(eng, id(ev[0]))] = ev
        for k in writes:
            st = self._touch(self._norm(k))
            st[0] = ev
            st[1] = {}

    def _waits(self, eng, deps):
        need = {}
        for (s, v) in deps:
            if s is self.esem[eng]:
                if eng in ("pe", "sp") or self.evsize.get((id(s), v), 0) >= SAME_ENG_BIG:
                    continue
            if self.known[eng].get(id(s), 0) >= v:
                continue
            if id(s) not in need or need[id(s)][1] < v:
                need[id(s)] = (s, v)
        for (s, v) in need.values():
            self.known[eng][id(s)] = v
        return list(need.values())

    def op(self, eng, fn, reads=(), writes=(), n=0):
        psk = [k[:2] for k in list(reads) + list(writes) if isinstance(k, tuple) and k[0] == "ps"]
        if psk:
            reads = [k for k in reads if not (isinstance(k, tuple) and k[0] == "ps")]
            writes = [k for k in writes if not (isinstance(k, tuple) and k[0] == "ps")] + list(dict.fromkeys(psk))
        deps = self._deps(reads, writes)
        waits = self._waits(eng, deps)
        if self.ecnt[eng] >= SEM_LIMIT:
            self.esem[eng] = self._newsem()
            self.ecnt[eng] = 0
        self.ecnt[eng] += 1
        ev = (self.esem[eng], self.ecnt[eng])
        self.evsize[(id(ev[0]), ev[1])] = n
        self.streams[eng].append((waits, fn, ev[0], 1))
        self._record(eng, ev, reads, writes)
        self.nops += 1
        return ev

    def dma(self, q, fn, reads=(), writes=()):
        deps = self._deps(reads, writes)
        i = self.dnext[q]
        self.dnext[q] += 1
        slot = self.dslots[q][i % DMA_SLOTS]
        if slot[1] > 0:
            deps.append((slot[0], slot[1]))
        w = self._waits(q, deps)
        if slot[1] + 16 > SEM_LIMIT:
            slot[0] = self._newsem()
            slot[1] = 0
        slot[1] += 16
        ev = (slot[0], slot[1])
        self.streams[q].append((w, fn, ev[0], 16))
        self._record(q, ev, reads, writes)
        self.nops += 1
        return ev

    def barrier(self):
        evs = []
        for e in ENGS:
            if self.ecnt[e] > 0:
                evs.append((self.esem[e], self.ecnt[e]))
        for q in self.dslots:
            for s, v in self.dslots[q]:
                if v > 0:
                    evs.append((s, v))
        for e in ENGS:
            w = self._waits(e, evs)
            if w:
                self.streams[e].append((w, None, None, 0))
        self.state = {}
        self.children = {}

    def flush(self):
        self.barrier()
        nc = self.nc
        streams = self.streams
        self.streams = {e: [] for e in ENGS}
        with nc.Block() as block:
            def mk(ename):
                def body(eng):
                    for (waits, fn, sem, inc) in streams[ename]:
                        for (s, v) in waits:
                            eng.wait_ge(s, v)
                        if fn is not None:
                            fn(eng).then_inc(sem, inc)
                return body
            block.tensor(mk("pe"))
            block.scalar(mk("act"))
            block.vector(mk("dve"))
            block.gpsimd(mk("pool"))
            block.sync(mk("sp"))

    def close(self):
        self.es.close()


R_XBC, R_S5, R_SC, R_CF, R_GATE, R_END = 0, 1280, 1792, 3328, 4352, 8448
OFF_XBC, OFF_DT, OFF_S5 = 768, 2048, 2060
YA, YB, YC, YD = 0, 768, 1280, 1792


def wcol(r):
    return OFF_XBC + r if r < 1280 else OFF_S5 + (r - 1280)


def build(dbg=False, phases=None, nlayers=L):
    nc = bass.Bass("TRN2", target_bir_lowering=False)
    P = Prog(nc)
    kin = "ExternalInput"

    def din(name, shape, dt=F32):
        return nc.dram_tensor(name, list(shape), dt, kind=kin).ap()

    def dscr(name, shape, dt=F32):
        return nc.dram_tensor(name, list(shape), dt, kind=("ExternalOutput" if dbg else "Internal")).ap()

    x_in = din("x", [S, D])
    g1 = din("norm1_g", [L, D]); g2 = din("norm2_g", [L, D]); gfin = din("final_norm_g", [1, D])
    w_in = din("w_in", [L, D, INC])
    ssd_cw = din("ssd_cw", [L, 128, 10, 4]); ssd_cb = din("ssd_cb", [L, 128, 10])
    ssd_dtb = din("ssd_dt_bias", [L, 12]); ssd_alog = din("ssd_a_log", [L, 12]); ssd_dsk = din("ssd_d", [L, 12])
    ssd_ng = din("ssd_norm_g", [L, 768])
    s5_lre = din("s5_lre", [L, 128, 16]); s5_lim = din("s5_lim", [L, 128, 16]); s5_lst = din("s5_lst", [L, 128, 16])
    s5_Bre = din("s5_Bre", [L, 128, 16, 128]); s5_Bim = din("s5_Bim", [L, 128, 16, 128])
    s5_Cre = din("s5_Cre", [L, 128, 16, 128]); s5_Cim = din("s5_Cim", [L, 128, 16, 128])
    s5_dsk = din("s5_dsk", [L, 128, 4]); s5_wglu = din("s5_w_glu", [L, 512, 512])
    sc_cw = din("sc_cw", [L, 128, 4, 3])
    cf_cw = din("cf_cw", [L, 128, 4, 31]); cf_g = din("cf_g", [L, 128, 4]); cf_b = din("cf_b", [L, 128, 4])
    w_branch = din("w_branch", [L, 2304, D]); w_out = din("w_out", [L, D, D])
    wq = din("peer_w_query", [L, D, 2048]); skT = din("skT", [L, 2, 128, 128])
    peer_u = [din("peer_u%d" % i, [16384, D]) for i in range(L)]
    peer_v = [din("peer_v%d" % i, [16384, D]) for i in range(L)]
    c_ident = din("c_ident", [128, 128]); c_triu = din("c_triu", [128, 128]); c_ones = din("c_ones", [128, 128])
    c_iota_t = din("c_iota_t", [128, S]); c_iota16 = din("c_iota16", [128, 16])

    y_out = nc.dram_tensor("y", [S, D], F32, kind="ExternalOutput").ap()
    xres = dscr("xres", [S, D])
    projT = dscr("projT", [R_END, S])
    z_tm = dscr("z_tm", [S, 768]); dt_tm = dscr("dt_tm", [S, 12])
    yT = dscr("yT", [2304, S])
    hTd = dscr("hTd", [D, S])

    ps = [nc.alloc_psum_tensor("ps%d" % i, [128, 512], F32) for i in range(8)]

    def DMA(q, out, in_, reads=(), writes=()):
        P.dma(q, lambda e: e.dma_start(out=out, in_=in_), reads, writes)

    def nel(ap, kw=None):
        if kw and kw.get("accum_out") is not None:
            return 0
        return int(np.prod(ap.shape[1:]))

    def ACT(out, in_, func, reads=(), writes=(), **kw):
        P.op("act", lambda e: e.activation(out=out, in_=in_, func=func, **kw), reads, writes, n=nel(out, kw))

    def TT(out, in0, in1, op, reads=(), writes=(), eng="dve"):
        P.op(eng, lambda e: e.tensor_tensor(out=out, in0=in0, in1=in1, op=op), reads, writes, n=nel(out))

    def TS(out, in0, s1, s2, op0, op1=None, reads=(), writes=(), eng="dve", **kw):
        if op1 is None:
            P.op(eng, lambda e: e.tensor_scalar(out=out, in0=in0, scalar1=s1, scalar2=None, op0=op0, **kw), reads, writes, n=nel(out, kw))
        else:
            P.op(eng, lambda e: e.tensor_scalar(out=out, in0=in0, scalar1=s1, scalar2=s2, op0=op0, op1=op1, **kw), reads, writes, n=nel(out, kw))

    def STT(out, in0, sc, in1, op0, op1, reads=(), writes=(), **kw):
        P.op("dve", lambda e: e.scalar_tensor_tensor(out=out, in0=in0, scalar=sc, in1=in1, op0=op0, op1=op1, **kw), reads, writes, n=nel(out, kw))

    def CP(out, in_, reads=(), writes=(), eng="dve"):
        P.op(eng, lambda e: e.tensor_copy(out=out, in_=in_), reads, writes, n=nel(out))

    def MM(out, lhsT, rhs, start, stop, reads=(), writes=()):
        P.op("pe", lambda e: e.matmul(out, lhsT=lhsT, rhs=rhs, start=start, stop=stop), reads, writes)

    def TR(out, in_, ident, reads=(), writes=()):
        P.op("pe", lambda e: e.transpose(out=out, in_=in_, identity=ident), reads, writes)

    def MEMSET(ap, v, writes=(), eng="dve"):
        P.op(eng, lambda e: e.memset(ap, v), (), writes, n=nel(ap))

    def DUMP(name, ap, shape, reads, dt=F32):
        if not dbg:
            return
        o = nc.dram_tensor("dbg_" + name, list(shape), dt, kind="ExternalOutput").ap()
        DMA("sp", o, ap, reads=reads)

    def RECIP(out, in_, reads=(), writes=()):
        P.op("dve", lambda e: e.reciprocal(out=out, in_=in_), reads, writes, n=nel(out))

    class Ph:
        uid = 0
        def __init__(self):
            self.es = ExitStack()
            self.n = 0
        def t(self, name, shape, dt=F32):
            Ph.uid += 1
            return self.es.enter_context(nc.sbuf_tensor("%s_%d" % (name, Ph.uid), list(shape), dt))
        def done(self):
            P.flush()
            self.es.close()

    def want(p):
        return phases is None or p in phases

    def load_consts(ph, names):
        out = {}
        for n in names:
            src = {"ident": c_ident, "triu": c_triu, "ones": c_ones, "iota_t": c_iota_t, "iota16": c_iota16}[n]
            t = ph.t("c_" + n, src.shape)
            DMA("sp", t[:], src[:, :], writes=["c_" + n])
            out[n] = t
        return out

    def rmsnorm_rows(ph, xt, gbc, ht, junk, ss, rs, eps_t, n, kx, kh):
        ACT(junk, xt, AF.Square, reads=[kx], writes=["junk", "ss"], accum_out=ss[:, 0:1])
        ACT(rs[:, 0:1], ss[:, 0:1], AF.Sqrt, reads=["ss", "eps"], writes=["rs"], bias=eps_t[:, 0:1], scale=1.0 / n)
        RECIP(rs[:, 0:1], rs[:, 0:1], reads=["rs"], writes=["rs"])
        STT(ht, xt, rs[:, 0:1], gbc, ALU.mult, ALU.mult, reads=[kx, "rs", "gbc"], writes=[kh])

    for l in range(nlayers):
        xsrc = x_in if l == 0 else xres

        if want("proj"):
            ph = Ph()
            C = load_consts(ph, ["ident"])
            hT = ph.t("hT", [128, 8, S])
            gbc = ph.t("gbc", [128, D])
            eps_t = ph.t("eps_t", [128, 1])
            junk = ph.t("junk", [128, D])
            ss = ph.t("ss", [128, 1]); rs = ph.t("rs", [128, 1])
            xt = [ph.t("xt%d" % i, [128, D]) for i in range(2)]
            ht = [ph.t("ht%d" % i, [128, D]) for i in range(2)]
            wt = [ph.t("wt%d" % i, [128, 8, 128]) for i in range(3)]
            ot = [ph.t("ot%d" % i, [128, S]) for i in range(2)]
            wz = ph.t("wz", [128, 8, 768]); wdt = ph.t("wdt", [128, 8, 12])
            zo = [ph.t("zo%d" % i, [128, 780]) for i in range(2)]
            MEMSET(eps_t[:], EPS, writes=["eps"])
            DMA("sp", gbc[:], g1[l:l + 1, :].partition_broadcast(128), writes=["gbc"])
            DMA("sp", wz[:], w_in[l, :, 0:768].rearrange("(k p) c -> p k c", p=128), writes=["wz"])
            DMA("sp", wdt[:], w_in[l, :, OFF_DT:OFF_DT + 12].rearrange("(k p) c -> p k c", p=128), writes=["wdt"])
            for tt in range(NT):
                b = tt % 2
                DMA("sp", xt[b][:], xsrc[tt * 128:(tt + 1) * 128, :], reads=["xres"], writes=[("xt", b)])
                rmsnorm_rows(ph, xt[b][:], gbc[:], ht[b][:], junk[:], ss, rs, eps_t, D, ("xt", b), ("ht", b))
                for half in range(2):
                    for k in range(4):
                        kk = half * 4 + k
                        TR(ps[half][:, k * 128:(k + 1) * 128], ht[b][:, kk * 128:(kk + 1) * 128], C["ident"][:],
                           reads=[("ht", b), "c_ident"], writes=[("ps", half)])
                    dst = hT[:, half * 4:(half + 1) * 4, tt * 128:(tt + 1) * 128]
                    src = ps[half][:, :].rearrange("p (k t) -> p k t", k=4)
                    if half == 0:
                        ACT(dst, src, AF.Copy, reads=[("ps", half)], writes=[("hT", tt)])
                    else:
                        CP(dst, src, reads=[("ps", half)], writes=[("hT", tt)])
                DMA("pool", hTd[:, tt * 128:(tt + 1) * 128].rearrange("(k p) t -> p k t", p=128), hT[:, :, tt * 128:(tt + 1) * 128],
                    reads=[("hT", tt)], writes=["hTd"])
                zb = zo[tt % 2]
                for (c0, c1, pb) in ((0, 512, 2), (512, 768, 3)):
                    for k in range(8):
                        MM(ps[pb][:, 0:c1 - c0], hT[:, k, tt * 128:(tt + 1) * 128], wz[:, k, c0:c1], k == 0, k == 7,
                           reads=[("hT", tt), "wz"], writes=[("ps", pb)])
                    ACT(zb[:, c0:c1], ps[pb][:, 0:c1 - c0], AF.Copy, reads=[("ps", pb)], writes=[("zo", tt % 2, c0)])
                for k in range(8):
                    MM(ps[3][:, 256:268], hT[:, k, tt * 128:(tt + 1) * 128], wdt[:, k, :], k == 0, k == 7,
                       reads=[("hT", tt), "wdt"], writes=[("ps", 3, "dt")])
                CP(zb[:, 768:780], ps[3][:, 256:268], reads=[("ps", 3, "dt")], writes=[("zo", tt % 2, 768)])
                DMA("pool", z_tm[tt * 128:(tt + 1) * 128, :], zb[:, 0:768], reads=[("zo", tt % 2)], writes=["z_tm"])
                DMA("pool", dt_tm[tt * 128:(tt + 1) * 128, :], zb[:, 768:780], reads=[("zo", tt % 2)], writes=["dt_tm"])
            nct = R_GATE // 128
            for ct in range(nct):
                wb_ = ct % 3
                c0 = wcol(ct * 128)
                DMA("sp", wt[wb_][:], w_in[l, :, c0:c0 + 128].rearrange("(k p) c -> p k c", p=128), writes=[("wt", wb_)])
                ob = ct % 2
                for tc in range(4):
                    pb = 4 + (ct * 4 + tc) % 4
                    for k in range(8):
                        MM(ps[pb][:, :], wt[wb_][:, k, :], hT[:, k, tc * 512:(tc + 1) * 512], k == 0, k == 7,
                           reads=[("wt", wb_), "hT"], writes=[("ps", pb)])
                    if tc % 2 == 0:
                        ACT(ot[ob][:, tc * 512:(tc + 1) * 512], ps[pb][:, :], AF.Copy, reads=[("ps", pb)], writes=[("ot", ob, tc)])
                    else:
                        CP(ot[ob][:, tc * 512:(tc + 1) * 512], ps[pb][:, :], reads=[("ps", pb)], writes=[("ot", ob, tc)])
                DMA("pool", projT[ct * 128:(ct + 1) * 128, :], ot[ob][:], reads=[("ot", ob)], writes=["projT"])
            ph.done()

        if want("s5"):
            ph = Ph()
            C = load_consts(ph, ["iota_t"])
            iota = C["iota_t"]
            lre = ph.t("lre", [128, 16]); lim = ph.t("lim", [128, 16]); lst = ph.t("lst", [128, 16])
            DMA("sp", lre[:], s5_lre[l], writes=["lre"])
            DMA("sp", lim[:], s5_lim[l], writes=["lim"])
            DMA("sp", lst[:], s5_lst[l], writes=["lst"])
            Bre = ph.t("Bre", [128, 16, 128]); Bim = ph.t("Bim", [128, 16, 128])
            Cre = ph.t("Cre", [128, 16, 128]); Cim = ph.t("Cim", [128, 16, 128])
            Ctr = ph.t("Ctr", [128, 16, 128]); Cti = ph.t("Cti", [128, 16, 128])
            DMA("sp", Bre[:], s5_Bre[l], writes=["Bre"]); DMA("sp", Bim[:], s5_Bim[l], writes=["Bim"])
            DMA("sp", Cre[:], s5_Cre[l], writes=["Cre"]); DMA("sp", Cim[:], s5_Cim[l], writes=["Cim"])
            dsk = ph.t("s5dsk", [128, 4]); wgl = ph.t("wgl", [128, 4, 512])
            DMA("sp", dsk[:], s5_dsk[l], writes=["dsk"])
            DMA("sp", wgl[:], s5_wglu[l].rearrange("(k p) c -> p k c", p=128), writes=["wgl"])
            sm = {n: ph.t("s5_" + n, [128, 16]) for n in
                  ("dl", "are", "th", "r", "k", "phi", "sn", "cs", "th2", "nr", "ni", "den", "kre", "kim", "t1", "t2")}
            K_ = "s5small"
            ACT(sm["dl"][:], lst[:], AF.Exp, reads=["lst"], writes=[K_])
            TT(sm["are"][:], lre[:], sm["dl"][:], ALU.mult, reads=["lre", K_], writes=[K_])
            TT(sm["th"][:], lim[:], sm["dl"][:], ALU.mult, reads=["lim", K_], writes=[K_])
            ACT(sm["r"][:], sm["are"][:], AF.Exp, reads=[K_], writes=[K_])

            TWO_PI_LO = 6.283185
            MAGIC = 12582912.0
            halfpi = ph.t("s5_halfpi", [128, 1])
            MEMSET(halfpi[:], 1.5707963, writes=["halfpi"])

            def sincos_small(th, out_s, out_c):
                TS(sm["k"][:], th, 1.0 / TWO_PI, None, ALU.mult, reads=[K_], writes=[K_])
                TS(sm["phi"][:], sm["k"][:], MAGIC, None, ALU.add, reads=[K_], writes=[K_])
                STT(sm["phi"][:], sm["phi"][:], MAGIC, sm["k"][:], ALU.subtract, ALU.subtract, reads=[K_], writes=[K_])
                ACT(out_s, sm["phi"][:], AF.Sin, reads=[K_], writes=[K_], scale=-TWO_PI_LO)
                STT(sm["th2"][:], sm["phi"][:], -1.0, sm["phi"][:], ALU.mult, ALU.max, reads=[K_], writes=[K_])
                ACT(out_c, sm["th2"][:], AF.Sin, reads=[K_, "halfpi"], writes=[K_], scale=-TWO_PI_LO, bias=halfpi[:, 0:1])

            sincos_small(sm["th"][:], sm["sn"][:], sm["cs"][:])
            TS(sm["t2"][:], sm["th"][:], 1.0 / TWO_PI, None, ALU.mult, reads=[K_], writes=[K_])
            th2pi = ph.t("s5_th2pi", [128, 16])
            CP(th2pi[:], sm["t2"][:], reads=[K_], writes=["th2pi"])
            TT(sm["nr"][:], sm["r"][:], sm["cs"][:], ALU.mult, reads=[K_], writes=[K_])
            TS(sm["nr"][:], sm["nr"][:], -1.0, None, ALU.add, reads=[K_], writes=[K_])
            TT(sm["ni"][:], sm["r"][:], sm["sn"][:], ALU.mult, reads=[K_], writes=[K_])
            TT(sm["den"][:], lre[:], lre[:], ALU.mult, reads=["lre", K_], writes=[K_])
            TT(sm["t1"][:], lim[:], lim[:], ALU.mult, reads=["lim", K_], writes=[K_])
            TT(sm["den"][:], sm["den"][:], sm["t1"][:], ALU.add, reads=[K_], writes=[K_])
            RECIP(sm["den"][:], sm["den"][:], reads=[K_], writes=[K_])
            TT(sm["t1"][:], sm["nr"][:], lre[:], ALU.mult, reads=[K_, "lre"], writes=[K_])
            TT(sm["t2"][:], sm["ni"][:], lim[:], ALU.mult, reads=[K_, "lim"], writes=[K_])
            TT(sm["t1"][:], sm["t1"][:], sm["t2"][:], ALU.add, reads=[K_], writes=[K_])
            TT(sm["kre"][:], sm["t1"][:], sm["den"][:], ALU.mult, reads=[K_], writes=[K_])
            TT(sm["t1"][:], sm["ni"][:], lre[:], ALU.mult, reads=[K_, "lre"], writes=[K_])
            TT(sm["t2"][:], sm["nr"][:], lim[:], ALU.mult, reads=[K_, "lim"], writes=[K_])
            TT(sm["t1"][:], sm["t1"][:], sm["t2"][:], ALU.subtract, reads=[K_], writes=[K_])
            TT(sm["kim"][:], sm["t1"][:], sm["den"][:], ALU.mult, reads=[K_], writes=[K_])
            ctmp = ph.t("ctmp", [128, 128])
            for tp in range(16):
                TS(Ctr[:, tp, :], Cre[:, tp, :], sm["kre"][:, tp:tp + 1], None, ALU.mult, reads=["Cre", K_], writes=[("Ctr", tp)])
                TS(ctmp[:], Cim[:, tp, :], sm["kim"][:, tp:tp + 1], None, ALU.mult, reads=["Cim", K_], writes=["ctmp"])
                TT(Ctr[:, tp, :], Ctr[:, tp, :], ctmp[:], ALU.subtract, reads=[("Ctr", tp), "ctmp"], writes=[("Ctr", tp)])
                TS(Cti[:, tp, :], Cre[:, tp, :], sm["kim"][:, tp:tp + 1], -1.0, ALU.mult, ALU.mult, reads=["Cre", K_], writes=[("Cti", tp)])
                TS(ctmp[:], Cim[:, tp, :], sm["kre"][:, tp:tp + 1], None, ALU.mult, reads=["Cim", K_], writes=["ctmp"])
                TT(Cti[:, tp, :], Cti[:, tp, :], ctmp[:], ALU.subtract, reads=[("Cti", tp), "ctmp"], writes=[("Cti", tp)])

            for n_ in ("th", "r", "kre", "kim", "sn", "cs"):
                DUMP("sm_" + n_, sm[n_][:], [128, 16], [K_])
            uT = ph.t("uT", [128, S])
            cs = ph.t("s5cos", [128, S]); sn = ph.t("s5sin", [128, S])
            x1 = ph.t("s5x1", [128, S]); kf = ph.t("s5kf", [128, S])
            vre = ph.t("vre", [128, S]); vim = ph.t("vim", [128, S])
            gre = ph.t("gre", [128, S]); gim = ph.t("gim", [128, S])
            yg = ph.t("yg", [128, 4, S])
            t1 = [ph.t("s5t1_%d" % i, [128, 512]) for i in range(2)]
            t2 = [ph.t("s5t2_%d" % i, [128, 512]) for i in range(2)]

            def sincos_big(tp):
                TS(x1[:], iota[:], th2pi[:, tp:tp + 1], None, ALU.mult, reads=["c_iota_t", "th2pi"], writes=["x1"])
                TS(kf[:], x1[:], MAGIC, None, ALU.add, reads=["x1"], writes=["kf"])
                STT(kf[:], kf[:], MAGIC, x1[:], ALU.subtract, ALU.subtract, reads=["kf", "x1"], writes=["kf"])
                ACT(sn[:], kf[:], AF.Sin, reads=["kf"], writes=["sn"], scale=-TWO_PI_LO)
                STT(x1[:], kf[:], -1.0, kf[:], ALU.mult, ALU.max, reads=["kf"], writes=["x1"])
                ACT(cs[:], x1[:], AF.Sin, reads=["x1", "halfpi"], writes=["cs"], scale=-TWO_PI_LO, bias=halfpi[:, 0:1])

            for ct in range(4):
                DMA("sp", uT[:], projT[R_S5 + ct * 128:R_S5 + (ct + 1) * 128, :], reads=["projT"], writes=["uT"])
                for tpi in range(4):
                    tp = ct * 4 + tpi
                    sincos_big(tp)
                    for tc in range(4):
                        sl = slice(tc * 512, (tc + 1) * 512)
                        pa, pb = ps[4 + (tc % 2) * 2], ps[5 + (tc % 2) * 2]
                        ka, kb = ("ps", 4 + (tc % 2) * 2), ("ps", 5 + (tc % 2) * 2)
                        MM(pa[:, :], Bre[:, tp, :], uT[:, sl], True, True, reads=["Bre", "uT"], writes=[ka])
                        MM(pb[:, :], Bim[:, tp, :], uT[:, sl], True, True, reads=["Bim", "uT"], writes=[kb])
                        a, b2 = t1[tc % 2], t2[tc % 2]
                        TT(a[:], pa[:, :], cs[:, sl], ALU.mult, reads=[ka, "cs"], writes=[("t1", tc % 2)])
                        TT(b2[:], pb[:, :], sn[:, sl], ALU.mult, reads=[kb, "sn"], writes=[("t2", tc % 2)])
                        TT(vre[:, sl], a[:], b2[:], ALU.add, reads=[("t1", tc % 2), ("t2", tc % 2)], writes=[("vre", tc)])
                        TT(a[:], pb[:, :], cs[:, sl], ALU.mult, reads=[kb, "cs"], writes=[("t1", tc % 2)])
                        TT(b2[:], pa[:, :], sn[:, sl], ALU.mult, reads=[ka, "sn"], writes=[("t2", tc % 2)])
                        TT(vim[:, sl], a[:], b2[:], ALU.subtract, reads=[("t1", tc % 2), ("t2", tc % 2)], writes=[("vim", tc)])
                    if tp == 0:
                        DUMP("sn", sn[:], [128, S], ["sn"]); DUMP("cs", cs[:], [128, S], ["cs"])
                        DUMP("vre", vre[:], [128, S], ["vre"]); DUMP("vim", vim[:], [128, S], ["vim"])
                    rb = sm["r"][:, tp:tp + 1].to_broadcast([128, S])
                    P.op("dve", lambda e, rb=rb: e.tensor_tensor_scan(out=gre[:], data0=rb, data1=vre[:], initial=0.0, op0=ALU.mult, op1=ALU.add),
                         reads=["vre", K_], writes=["gre"])
                    P.op("dve", lambda e, rb=rb: e.tensor_tensor_scan(out=gim[:], data0=rb, data1=vim[:], initial=0.0, op0=ALU.mult, op1=ALU.add),
                         reads=["vim", K_], writes=["gim"])
                    if tp == 0:
                        DUMP("gre", gre[:], [128, S], ["gre"]); DUMP("gim", gim[:], [128, S], ["gim"])
                    TT(vre[:], cs[:], gre[:], ALU.mult, reads=["cs", "gre"], writes=["vre"])
                    TT(x1[:], sn[:], gim[:], ALU.mult, reads=["sn", "gim"], writes=["x1"])
                    TT(vre[:], vre[:], x1[:], ALU.subtract, reads=["vre", "x1"], writes=["vre"])
                    TT(vim[:], sn[:], gre[:], ALU.mult, reads=["sn", "gre"], writes=["vim"])
                    TT(x1[:], cs[:], gim[:], ALU.mult, reads=["cs", "gim"], writes=["x1"])
                    TT(vim[:], vim[:], x1[:], ALU.add, reads=["vim", "x1"], writes=["vim"])
                    for tc in range(4):
                        sl = slice(tc * 512, (tc + 1) * 512)
                        MM(ps[tc][:, :], Ctr[:, tp, :], vre[:, sl], tpi == 0, False, reads=[("Ctr", tp), "vre"], writes=[("ps", tc)])
                        MM(ps[tc][:, :], Cti[:, tp, :], vim[:, sl], False, tpi == 3, reads=[("Cti", tp), "vim"], writes=[("ps", tc)])
                for tc in range(4):
                    sl = slice(tc * 512, (tc + 1) * 512)
                    STT(yg[:, ct, sl], uT[:, sl], dsk[:, ct:ct + 1], ps[tc][:, :], ALU.mult, ALU.add,
                        reads=["uT", "dsk", ("ps", tc)], writes=[("yg", ct, tc)])
                if ct == 0:
                    DUMP("ypre", yg[:, 0, :], [128, S], [("yg", 0)])
                ACT(yg[:, ct, :], yg[:, ct, :], AF.Gelu_apprx_tanh, reads=[("yg", ct)], writes=[("yg", ct)])
            for co in range(4):
                for tc in range(4):
                    sl = slice(tc * 512, (tc + 1) * 512)
                    pb = 4 + (co * 4 + tc) % 4
                    for ci in range(4):
                        MM(ps[pb][:, :], wgl[:, ci, co * 128:(co + 1) * 128], yg[:, ci, sl], ci == 0, ci == 3,
                           reads=["wgl", "yg"], writes=[("ps", pb)])
                    tb = (co * 4 + tc) % 2
                    ACT(t1[tb][:], ps[pb][:, :], AF.Sigmoid, reads=[("ps", pb)], writes=[("t1", tb)])
                    TT(t1[tb][:], t1[tb][:], yg[:, co, sl], ALU.mult, reads=[("t1", tb), "yg"], writes=[("t1", tb)])
                    DMA("pool", yT[YB + co * 128:YB + (co + 1) * 128, sl], t1[tb][:], reads=[("t1", tb)], writes=["yT"])
            ph.done()

        gph = None
        gen = iter(())
        if want("sc") or want("cf") or want("ssd"):
            gph = Ph()
            ghT = [gph.t("ghT%d" % i, [128, 8, 512]) for i in range(2)]
            gwt = [gph.t("gwt%d" % i, [128, 8, 128]) for i in range(3)]
            gog = [gph.t("gog%d" % i, [128, 512]) for i in range(3)]

            def gate_groups():
                g = 0
                for tc in range(4):
                    hb = tc % 2
                    DMA("pool", ghT[hb][:], hTd[:, tc * 512:(tc + 1) * 512].rearrange("(k p) t -> p k t", p=128),
                        reads=["hTd"], writes=[("ghT", hb)])
                    for ct in range(R_GATE // 128, R_END // 128):
                        wb_ = g % 3
                        pb = 6 + g % 2
                        c0 = wcol(ct * 128)
                        DMA("pool", gwt[wb_][:], w_in[l, :, c0:c0 + 128].rearrange("(k p) c -> p k c", p=128), writes=[("gwt", wb_)])
                        for k in range(8):
                            MM(ps[pb][:, :], gwt[wb_][:, k, :], ghT[hb][:, k, :], k == 0, k == 7,
                               reads=[("gwt", wb_), ("ghT", hb)], writes=[("ps", pb)])
                        ACT(gog[wb_][:], ps[pb][:, :], AF.Copy, reads=[("ps", pb)], writes=[("gog", wb_)])
                        DMA("act", projT[ct * 128:(ct + 1) * 128, tc * 512:(tc + 1) * 512], gog[wb_][:], reads=[("gog", wb_)], writes=["projTg"])
                        g += 1
                        yield
            gen = gate_groups()

        def pump(n=1):
            for _ in range(n):
                next(gen, None)

        if want("sc"):
            ph = Ph()
            cw = ph.t("sccw", [128, 4, 3])
            DMA("sp", cw[:], sc_cw[l], writes=["cw"])
            bt = [ph.t("scb%d" % i, [128, S]) for i in range(2)]
            ctt = [ph.t("scc%d" % i, [128, S]) for i in range(2)]
            htt = [ph.t("sch%d" % i, [128, S]) for i in range(2)]
            chp = [ph.t("chp%d" % i, [128, S + 2]) for i in range(2)]
            acc = [ph.t("sca%d" % i, [128, S]) for i in range(2)]
            for i in range(2):
                MEMSET(chp[i][:, 0:2], 0.0, writes=[("chp", i, "h")])
            for ct in range(4):
                b = ct % 2
                r0 = R_SC + ct * 128
                DMA("sp", bt[b][:], projT[r0:r0 + 128, :], reads=["projT"], writes=[("bt", b)])
                DMA("sp", ctt[b][:], projT[r0 + 512:r0 + 640, :], reads=["projT"], writes=[("ct", b)])
                DMA("sp", htt[b][:], projT[r0 + 1024:r0 + 1152, :], reads=["projT"], writes=[("ht", b)])
                pump(3)
                TT(chp[b][:, 2:S + 2], ctt[b][:], htt[b][:], ALU.mult, reads=[("ct", b), ("ht", b)], writes=[("chp", b, "d")])
                TS(acc[b][:], chp[b][:, 0:S], cw[:, ct, 0:1], None, ALU.mult, reads=[("chp", b), "cw"], writes=[("acc", b)])
                for k in (1, 2):
                    STT(acc[b][:], chp[b][:, k:k + S], cw[:, ct, k:k + 1], acc[b][:], ALU.mult, ALU.add,
                        reads=[("chp", b), "cw", ("acc", b)], writes=[("acc", b)])
                TT(acc[b][:], acc[b][:], bt[b][:], ALU.mult, reads=[("acc", b), ("bt", b)], writes=[("acc", b)])
                DMA("act", yT[YC + ct * 128:YC + (ct + 1) * 128, :], acc[b][:], reads=[("acc", b)], writes=["yT"])
            ph.done()

        if want("cf"):
            ph = Ph()
            C = load_consts(ph, ["ones"])
            cw = ph.t("cfcw", [128, 4, 31]); cg = ph.t("cfg", [128, 4]); cb = ph.t("cfb", [128, 4])
            DMA("sp", cw[:], cf_cw[l], writes=["cw"])
            DMA("sp", cg[:], cf_g[l], writes=["cg"])
            DMA("sp", cb[:], cf_b[l], writes=["cb"])
            eps_t = ph.t("eps_t", [128, 1])
            MEMSET(eps_t[:], EPS, writes=["eps"])
            at = [ph.t("cfa%d" % i, [128, S]) for i in range(2)]
            gt = [ph.t("cfgt%d" % i, [128, S]) for i in range(2)]
            vp = [ph.t("cfv%d" % i, [128, S + 30]) for i in range(2)]
            conv = ph.t("cfconv", [128, 4, S])
            sq = [ph.t("cfsq%d" % i, [128, 512]) for i in range(2)]
            mean = ph.t("cfmean", [128, 512]); m2 = ph.t("cfm2", [128, 512]); rstd = ph.t("cfrstd", [128, 512])
            tmp = [ph.t("cftmp%d" % i, [128, 512]) for i in range(2)]
            for i in range(2):
                MEMSET(vp[i][:, 0:30], 0.0, writes=[("vp", i, "h")])
            for ct in range(4):
                b = ct % 2
                r0 = R_CF + ct * 128
                DMA("sp", at[b][:], projT[r0:r0 + 128, :], reads=["projT"], writes=[("at", b)])
                DMA("sp", gt[b][:], projT[r0 + 512:r0 + 640, :], reads=["projT"], writes=[("gt", b)])
                ACT(gt[b][:], gt[b][:], AF.Sigmoid, reads=[("gt", b)], writes=[("gt", b)])
                TT(vp[b][:, 30:S + 30], at[b][:], gt[b][:], ALU.mult, reads=[("at", b), ("gt", b)], writes=[("vp", b, "d")])
                TS(conv[:, ct, :], vp[b][:, 0:S], cw[:, ct, 0:1], None, ALU.mult, reads=[("vp", b), "cw"], writes=[("conv", ct)])
                for k in range(1, 31):
                    if k % 4 == 0:
                        pump(1)
                    STT(conv[:, ct, :], vp[b][:, k:k + S], cw[:, ct, k:k + 1], conv[:, ct, :], ALU.mult, ALU.add,
                        reads=[("vp", b), "cw", ("conv", ct)], writes=[("conv", ct)])
            for tc in range(4):
                sl = slice(tc * 512, (tc + 1) * 512)
                pump(1)
                for ct in range(4):
                    MM(ps[0][:, :], C["ones"][:], conv[:, ct, sl], ct == 0, ct == 3, reads=["c_ones", "conv"], writes=[("ps", 0)])
                for ct in range(4):
                    ACT(sq[ct % 2][:], conv[:, ct, sl], AF.Square, reads=["conv"], writes=[("sq", ct % 2)])
                    MM(ps[1][:, :], C["ones"][:], sq[ct % 2][:], ct == 0, ct == 3, reads=["c_ones", ("sq", ct % 2)], writes=[("ps", 1)])
                ACT(mean[:], ps[0][:, :], AF.Identity, reads=[("ps", 0)], writes=["mean"], scale=1.0 / 512)
                TT(m2[:], mean[:], mean[:], ALU.mult, reads=["mean"], writes=["m2"])
                STT(rstd[:], ps[1][:, :], 1.0 / 512, m2[:], ALU.mult, ALU.subtract, reads=[("ps", 1), "m2"], writes=["rstd"])
                ACT(rstd[:], rstd[:], AF.Sqrt, reads=["rstd", "eps"], writes=["rstd"], bias=eps_t[:, 0:1], scale=1.0)
                RECIP(rstd[:], rstd[:], reads=["rstd"], writes=["rstd"])
                for ct in range(4):
                    tb = ct % 2
                    TT(tmp[tb][:], conv[:, ct, sl], mean[:], ALU.subtract, reads=["conv", "mean"], writes=[("tmp", tb)])
                    TT(tmp[tb][:], tmp[tb][:], rstd[:], ALU.mult, reads=[("tmp", tb), "rstd"], writes=[("tmp", tb)])
                    ACT(tmp[tb][:], tmp[tb][:], AF.Silu, reads=[("tmp", tb), "cg", "cb"], writes=[("tmp", tb)],
                        scale=cg[:, ct:ct + 1], bias=cb[:, ct:ct + 1])
                    DMA("act", yT[YD + ct * 128:YD + (ct + 1) * 128, sl], tmp[tb][:], reads=[("tmp", tb)], writes=["yT"])
            ph.done()

        if want("ssd"):
            ph = Ph()
            C = load_consts(ph, ["ident", "triu", "ones"])
            ident, triu, ones = C["ident"], C["triu"], C["ones"]
            cw = ph.t("ssdcw", [128, 10, 4]); cbias = ph.t("ssdcb", [128, 10])
            DMA("sp", cw[:], ssd_cw[l], writes=["cw"]); DMA("sp", cbias[:], ssd_cb[l], writes=["cb"])
            dtb = ph.t("dtb", [128, 12]); Abc = ph.t("Abc", [128, 12]); Dbc = ph.t("Dbc", [128, 12]); ngbc = ph.t("ngbc", [128, 768])
            one_t = ph.t("one_t", [128, 1]); eps_t = ph.t("eps_t", [128, 1])
            MEMSET(one_t[:], 1.0, writes=["one"]); MEMSET(eps_t[:], EPS, writes=["eps"])
            DMA("sp", dtb[:], ssd_dtb[l:l + 1, :].partition_broadcast(128), writes=["dtb"])
            DMA("sp", Abc[:], ssd_alog[l:l + 1, :].partition_broadcast(128), writes=["Abc"])
            DMA("sp", Dbc[:], ssd_dsk[l:l + 1, :].partition_broadcast(128), writes=["Dbc"])
            DMA("sp", ngbc[:], ssd_ng[l:l + 1, :].partition_broadcast(128), writes=["ngbc"])
            ACT(Abc[:], Abc[:], AF.Exp, reads=["Abc"], writes=["Abc"])
            TS(Abc[:], Abc[:], -1.0, None, ALU.mult, reads=["Abc"], writes=["Abc"])
            xwin = [ph.t("xwin%d" % i, [128, 10, 131]) for i in range(2)]
            zt = [ph.t("zt%d" % i, [128, 768]) for i in range(2)]
            dtr = [ph.t("dtr%d" % i, [128, 12]) for i in range(2)]
            xc = ph.t("xc", [128, 10, 128])
            xtm = ph.t("xtm", [128, 768]); Btm = ph.t("Btm", [128, 256])
            dtv = ph.t("dtv", [128, 12]); av = ph.t("av", [128, 12]); ncum = ph.t("ncum", [128, 12])
            tria = ph.t("tria", [128, 12, 128])
            dl = ph.t("dl", [128, 12, 128]); ecum = ph.t("ecum", [128, 12, 128])
            GTm = ph.t("GTm", [128, 4, 128]); MT = ph.t("MT", [128, 12, 128])
            xdt = ph.t("xdt", [128, 12, 64]); xdtd = ph.t("xdtd", [128, 12, 64])
            CpT = ph.t("CpT", [128, 4, 3, 128])
            Cmask = ph.t("Cmask", [128, 4, 128])
            Sst = ph.t("Sst", [128, 2, 3, 64])
            y1 = ph.t("y1", [128, 768]); sz = ph.t("sz", [128, 768]); junk = ph.t("junk", [128, 768])
            ss = ph.t("ss", [128, 1]); rs = ph.t("rs", [128, 1])
            yTa = ph.t("yTa", [128, 6, 128])
            MEMSET(Sst[:], 0.0, writes=["Sst"])
            MEMSET(CpT[:], 0.0, writes=["CpT"])
            MEMSET(Cmask[:], 0.0, writes=["Cmask"])
            for i in range(2):
                MEMSET(xwin[i][:, :, 0:3], 0.0, writes=[("xwin", i)])
            for c in range(NT):
                b = c % 2
                t0 = c * 128
                if c == 0:
                    DMA("sp", xwin[b][:, :, 3:131], projT[0:1280, 0:128].rearrange("(t p) l -> p t l", p=128),
                        reads=["projT"], writes=[("xwin", b)])
                else:
                    DMA("sp", xwin[b][:, :, :], projT[0:1280, t0 - 3:t0 + 128].rearrange("(t p) l -> p t l", p=128),
                        reads=["projT"], writes=[("xwin", b)])
                DMA("sp", zt[b][:], z_tm[t0:t0 + 128, :], reads=["z_tm"], writes=[("zt", b)])
                DMA("sp", dtr[b][:], dt_tm[t0:t0 + 128, :], reads=["dt_tm"], writes=[("dtr", b)])
                TT(dtv[:], dtr[b][:], dtb[:], ALU.add, reads=[("dtr", b), "dtb"], writes=["dtv"])
                ACT(dtv[:], dtv[:], AF.Exp, reads=["dtv"], writes=["dtv"])
                ACT(dtv[:], dtv[:], AF.Ln, reads=["dtv", "one"], writes=["dtv"], bias=one_t[:, 0:1], scale=1.0)
                ACT(sz[:], zt[b][:], AF.Silu, reads=[("zt", b)], writes=["sz"])

                def conv_tiles(t_lo, t_hi):
                    for t in range(t_lo, t_hi):
                        TS(xc[:, t, :], xwin[b][:, t, 0:128], cw[:, t, 0:1], cbias[:, t:t + 1], ALU.mult, ALU.add,
                           reads=[("xwin", b), "cw", "cb"], writes=[("xc", t)])
                        for k in range(1, 4):
                            STT(xc[:, t, :], xwin[b][:, t, k:k + 128], cw[:, t, k:k + 1], xc[:, t, :], ALU.mult, ALU.add,
                                reads=[("xwin", b), "cw", ("xc", t)], writes=[("xc", t)])
                conv_tiles(0, 5)
                pump(1)
                TT(av[:], dtv[:], Abc[:], ALU.mult, reads=["dtv", "Abc"], writes=["av"])
                MM(ps[2][:, 0:12], triu[:], av[:], True, True, reads=["c_triu", "av"], writes=[("ps", 2)])
                TS(ncum[:], ps[2][:, 0:12], -1.0, None, ALU.mult, reads=[("ps", 2)], writes=["ncum"])
                for j in range(12):
                    TS(tria[:, j, :], triu[:], av[:, j:j + 1], None, ALU.mult, reads=["c_triu", "av"], writes=[("tria", j)])
                for q in range(3):
                    MM(ps[2 + q][:, :], ones[:], tria[:, q * 4:(q + 1) * 4, :].rearrange("p j l -> p (j l)"), True, True,
                       reads=["c_ones", "tria"], writes=[("ps", 2 + q)])
                conv_tiles(5, 10)
                ACT(xc[:], xc[:], AF.Silu, reads=["xc"], writes=["xc"])
                pump(1)
                for t in range(8):
                    pb = t // 4
                    TR(ps[pb][:, (t % 4) * 128:(t % 4 + 1) * 128], xc[:, t, :], ident[:], reads=["xc", "c_ident"], writes=[("ps", pb, t % 4)])
                ACT(xtm[:, 0:512], ps[0][:, :], AF.Copy, reads=[("ps", 0)], writes=["xtm"])
                ACT(xtm[:, 512:768], ps[1][:, 0:256], AF.Copy, reads=[("ps", 1)], writes=["xtm"])
                CP(Btm[:], ps[1][:, 256:512], reads=[("ps", 1)], writes=["Btm"])
                for q in range(3):
                    psv = ps[2 + q][:, :].rearrange("p (j l) -> p j l", j=4)
                    nb = ncum[:, q * 4:(q + 1) * 4].unsqueeze(2).to_broadcast([128, 4, 128])
                    TT(dl[:, q * 4:(q + 1) * 4, :], psv, nb, ALU.add, reads=[("ps", 2 + q), "ncum"], writes=[("dl", q)])
                    ACT(ecum[:, q * 4:(q + 1) * 4, :], psv, AF.Exp, reads=[("ps", 2 + q)], writes=[("ecum", q), ("ps", 2 + q)])
                TS(dl[:], dl[:], 0.0, None, ALU.min, reads=["dl"], writes=["dl"])
                ACT(dl[:], dl[:], AF.Exp, reads=["dl"], writes=["dl"])
                for g in range(4):
                    hp = (g % 2) * 64
                    CP(Cmask[hp:hp + 64, g, :], xc[hp:hp + 64, 8 + g // 2, :], reads=["xc"], writes=[("Cmask", g)])
                    MM(ps[5][:, g * 128:(g + 1) * 128], xc[:, 6 + g // 2, :], Cmask[:, g, :], True, True,
                       reads=["xc", ("Cmask", g)], writes=[("ps", 5, g)])
                TT(GTm[:], ps[5][:, :].rearrange("p (g l) -> p g l", g=4), triu[:].unsqueeze(1).to_broadcast([128, 4, 128]), ALU.mult,
                   reads=[("ps", 5), "c_triu"], writes=["GTm"])
                for g in range(4):
                    TT(MT[:, 3 * g:3 * g + 3, :], dl[:, 3 * g:3 * g + 3, :], GTm[:, g, :].unsqueeze(1).to_broadcast([128, 3, 128]), ALU.mult,
                       reads=["dl", "GTm"], writes=[("MT", g)])
                pump(1)
                xv = xtm[:].rearrange("p (j d) -> p j d", j=12)
                TT(xdt[:], xv, dtv[:].unsqueeze(2).to_broadcast([128, 12, 64]), ALU.mult, reads=["xtm", "dtv"], writes=["xdt"])
                TT(xdtd[:], xdt[:], dl[:, :, 127:128].to_broadcast([128, 12, 64]), ALU.mult, reads=["xdt", "dl"], writes=["xdtd"])
                for g in range(4):
                    hp = (g % 2) * 64
                    TT(CpT[hp:hp + 64, g, :, :], ecum[hp:hp + 64, 3 * g:3 * g + 3, :],
                       xc[hp:hp + 64, 8 + g // 2, :].unsqueeze(1).to_broadcast([64, 3, 128]), ALU.mult,
                       reads=["ecum", "xc"], writes=[("CpT", g)])
                TT(y1[:].rearrange("p (j d) -> p j d", j=12), xv, Dbc[:].unsqueeze(2).to_broadcast([128, 12, 64]), ALU.mult,
                   reads=["xtm", "Dbc"], writes=["y1"])
                for j in range(12):
                    g = j // 3
                    hp = (g % 2) * 64
                    pb, co = (3, j * 64) if j < 8 else (4, 256 + (j - 8) * 64)
                    key = ("ps", pb)
                    MM(ps[pb][:, co:co + 64], MT[:, j, :], xdt[:, j, :], True, False, reads=[("MT", g), "xdt"], writes=[key])
                    MM(ps[pb][:, co:co + 64], CpT[:, g, j % 3, :], Sst[:, g // 2, j % 3, :], False, True,
                       reads=[("CpT", g), "Sst"], writes=[key])
                pump(1)
                for m in range(2):
                    MM(ps[m][:, 0:384], Btm[:, m * 128:(m + 1) * 128], xdtd[:, 6 * m:6 * m + 6, :].rearrange("p j d -> p (j d)"), True, True,
                       reads=["Btm", "xdtd"], writes=[("ps", m)])
                pump(1)
                TT(y1[:, 0:512], y1[:, 0:512], ps[3][:, :], ALU.add, reads=["y1", ("ps", 3)], writes=["y1"])
                TT(y1[:, 512:768], y1[:, 512:768], ps[4][:, 256:512], ALU.add, reads=["y1", ("ps", 4)], writes=["y1"])
                TT(y1[:], y1[:], sz[:], ALU.mult, reads=["y1", "sz"], writes=["y1"])
                ACT(junk[:], y1[:], AF.Square, reads=["y1"], writes=["junk", "ss"], accum_out=ss[:, 0:1])
                ACT(rs[:, 0:1], ss[:, 0:1], AF.Sqrt, reads=["ss", "eps"], writes=["rs"], bias=eps_t[:, 0:1], scale=1.0 / 768)
                for m in range(2):
                    for gl in range(2):
                        g = 2 * m + gl
                        hp = gl * 64
                        eb = ecum[hp:hp + 64, 3 * g:3 * g + 3, 127:128].to_broadcast([64, 3, 64])
                        TT(Sst[hp:hp + 64, m, :, :], Sst[hp:hp + 64, m, :, :], eb, ALU.mult, reads=["Sst", "ecum"], writes=["Sst"])
                        TT(Sst[hp:hp + 64, m, :, :], Sst[hp:hp + 64, m, :, :],
                           ps[m][hp:hp + 64, gl * 192:gl * 192 + 192].rearrange("p (j d) -> p j d", j=3), ALU.add,
                           reads=["Sst", ("ps", m)], writes=["Sst"])
                RECIP(rs[:, 0:1], rs[:, 0:1], reads=["rs"], writes=["rs"])
                STT(y1[:], y1[:], rs[:, 0:1], ngbc[:], ALU.mult, ALU.mult, reads=["y1", "rs", "ngbc"], writes=["y1"])
                for t in range(6):
                    pb = 2 if t < 4 else 5
                    TR(ps[pb][:, (t % 4) * 128:(t % 4 + 1) * 128], y1[:, t * 128:(t + 1) * 128], ident[:], reads=["y1", "c_ident"], writes=[("ps", pb)])
                ACT(yTa[:, 0:4, :], ps[2][:, :].rearrange("p (t l) -> p t l", t=4), AF.Copy, reads=[("ps", 2)], writes=["yTa"])
                CP(yTa[:, 4:6, :], ps[5][:, 0:256].rearrange("p (t l) -> p t l", t=2), reads=[("ps", 5)], writes=["yTa"])
                DMA("act", yT[YA:YA + 768, t0:t0 + 128].rearrange("(t p) l -> p t l", p=128), yTa[:], reads=["yTa"], writes=["yT"])
            ph.done()

        if gph is not None:
            for _ in gen:
                pass
            gph.done()

        if want("merge"):
            ph = Ph()
            wout = ph.t("wout", [128, 8, D])
            DMA("sp", wout[:], w_out[l].rearrange("(k p) c -> p k c", p=128), writes=["wout"])
            ytc = [ph.t("ytc%d" % i, [128, 18, 512]) for i in range(2)]
            wb = [ph.t("wb%d" % i, [128, 18, 128]) for i in range(2)]
            gt = [ph.t("mg%d" % i, [128, 4, 512]) for i in range(2)]
            mT = ph.t("mT", [128, 8, 512])
            tmp = ph.t("mtmp", [128, 512])
            xo = [ph.t("mxo%d" % i, [128, D]) for i in range(2)]
            br_tiles = [(0, 6), (6, 10), (10, 14), (14, 18)]
            it = 0
            for tc in range(4):
                sl = slice(tc * 512, (tc + 1) * 512)
                yb = tc % 2
                DMA("sp", ytc[yb][:], yT[:, sl].rearrange("(t p) l -> p t l", p=128), reads=["yT"], writes=[("ytc", yb)])
                for dt_ in range(8):
                    b = it % 2
                    it += 1
                    DMA("sp", wb[b][:], w_branch[l, :, dt_ * 128:(dt_ + 1) * 128].rearrange("(t p) d -> p t d", p=128), writes=[("wb", b)])
                    DMA("sp", gt[b][:], projT[R_GATE:R_END, sl].rearrange("(b r) l -> r b l", b=4)[dt_ * 128:(dt_ + 1) * 128],
                        reads=["projT"], writes=[("gt", b)])
                    ACT(gt[b][:], gt[b][:], AF.Sigmoid, reads=[("gt", b)], writes=[("gt", b)])
                    for bi, (ta, tb_) in enumerate(br_tiles):
                        pb = 4 + bi
                        for t in range(ta, tb_):
                            MM(ps[pb][:, :], wb[b][:, t, :], ytc[yb][:, t, :], t == ta, t == tb_ - 1,
                               reads=[("wb", b), ("ytc", yb)], writes=[("ps", pb)])
                        if bi == 0:
                            TT(mT[:, dt_, :], ps[pb][:, :], gt[b][:, bi, :], ALU.mult, reads=[("ps", pb), ("gt", b)], writes=[("mT", dt_)])
                        else:
                            TT(tmp[:], ps[pb][:, :], gt[b][:, bi, :], ALU.mult, reads=[("ps", pb), ("gt", b)], writes=["tmp"])
                            TT(mT[:, dt_, :], mT[:, dt_, :], tmp[:], ALU.add, reads=[("mT", dt_), "tmp"], writes=[("mT", dt_)])
                for t4 in range(4):
                    tt = tc * 4 + t4
                    xb = tt % 2
                    DMA("sp", xo[xb][:], xsrc[tt * 128:(tt + 1) * 128, :], reads=["xres"], writes=[("xo", xb)])
                    for half in range(2):
                        pb = half
                        for k in range(8):
                            MM(ps[pb][:, :], mT[:, k, t4 * 128:(t4 + 1) * 128], wout[:, k, half * 512:(half + 1) * 512], k == 0, k == 7,
                               reads=["mT", "wout"], writes=[("ps", pb)])
                        TT(xo[xb][:, half * 512:(half + 1) * 512], xo[xb][:, half * 512:(half + 1) * 512], ps[pb][:, :], ALU.add,
                           reads=[("xo", xb), ("ps", pb)], writes=[("xo", xb)])
                    DMA("pool", xres[tt * 128:(tt + 1) * 128, :], xo[xb][:], reads=[("xo", xb)], writes=[("xres", tt)])
            ph.done()

        if want("peer"):
            ph = Ph()
            C = load_consts(ph, ["ident", "iota16"])
            ident, iota16 = C["ident"], C["iota16"]
            wqt = ph.t("wqt", [128, 8, 2048])
            DMA("sp", wqt[:], wq[l].rearrange("(k p) c -> p k c", p=128), writes=["wqt"])
            skt = ph.t("skt", [128, 2, 128])
            DMA("sp", skt[:], skT[l].rearrange("two d k -> d two k"), writes=["skt"])
            gbc = ph.t("gbc", [128, D])
            DMA("sp", gbc[:], g2[l:l + 1, :].partition_broadcast(128), writes=["gbc"])
            eps_t = ph.t("eps_t", [128, 1])
            MEMSET(eps_t[:], EPS, writes=["eps"])
            junk = ph.t("junk", [128, D]); junk2 = ph.t("junk2", [128, D]); ss = ph.t("ss", [128, 1]); rs = ph.t("rs", [128, 1])
            xt_ = [ph.t("pxt%d" % i, [128, D]) for i in range(2)]; h2_ = [ph.t("ph2%d" % i, [128, D]) for i in range(2)]; h2T = ph.t("h2T", [128, 8, 128])
            qT = ph.t("qT", [128, 16, 128]); sc = ph.t("psc", [128, 16, 128]); wk4 = [ph.t("pwk%d" % i, [128, 256]) for i in range(4)]
            mx = ph.t("pmx", [128, 16, 16]); mi = ph.t("pmi", [128, 16, 16], U16); mif = ph.t("pmif", [128, 16, 16])
            cand = ph.t("cand", [128, 8, 256])
            best = ph.t("best", [128, 8, 16]); pos = ph.t("ppos", [128, 8, 16], U16); posf = ph.t("posf", [128, 8, 16])
            ai = ph.t("pai", [128, 128], I32); af = ph.t("paf", [128, 128]); bf = ph.t("pbf", [128, 128])
            eq = ph.t("peq", [128, 8, 16, 16])
            i0s = ph.t("i0s", [128, 128]); i1s = ph.t("i1s", [128, 128]); idxf = ph.t("idxf", [128, 128])
            gate_ = [ph.t("gate%d" % i, [128, 8, 16]) for i in range(2)]; gs = ph.t("gs", [128, 8])
            idxi_ = [ph.t("idxi%d" % i, [128, 128], I32) for i in range(2)]
            AT = ph.t("AT", [128, 128]); actT = ph.t("actT", [128, 128])
            NB = 16
            ur = [ph.t("ur%d" % i, [128, D]) for i in range(NB)]
            def routeA(tt, bb):
                xt, h2, gate, idxi = xt_[bb], h2_[bb], gate_[bb], idxi_[bb]
                KX, KH, KG, KI = ("xt", bb), ("h2", bb), ("gate", bb), ("idxi", bb)
                t0 = tt * 128
                DMA("sp", xt[:], xres[t0:t0 + 128, :], reads=[("xres", tt)], writes=[KX])
                rmsnorm_rows(ph, xt[:], gbc[:], h2[:], junk[:], ss, rs, eps_t, D, KX, KH)
                for half in range(2):
                    for k in range(4):
                        kk = half * 4 + k
                        TR(ps[half][:, k * 128:(k + 1) * 128], h2[:, kk * 128:(kk + 1) * 128], ident[:], reads=[KH, "c_ident"], writes=[("ps", half)])
                    ACT(h2T[:, half * 4:(half + 1) * 4, :], ps[half][:, :].rearrange("p (k t) -> p k t", k=4), AF.Copy,
                        reads=[("ps", half)], writes=["h2T"])
                for hq in range(4):
                    pb = 2 + hq % 2
                    for j in range(4):
                        hp = hq * 4 + j
                        for k in range(8):
                            MM(ps[pb][:, j * 128:(j + 1) * 128], wqt[:, k, hp * 128:(hp + 1) * 128], h2T[:, k, :], k == 0, k == 7,
                               reads=["wqt", "h2T"], writes=[("ps", pb, j)])
                    ACT(qT[:, hq * 4:(hq + 1) * 4, :], ps[pb][:, :].rearrange("p (j t) -> p j t", j=4), AF.Copy, reads=[("ps", pb)], writes=[("qT", hq)])
                for hq in range(4):
                    pb = 4 + hq % 2
                    for j in range(4):
                        hp = hq * 4 + j
                        MM(ps[pb][:, j * 128:(j + 1) * 128], qT[:, hp, :], skt[:, hp % 2, :], True, True,
                           reads=[("qT", hq), "skt"], writes=[("ps", pb, j)])
                    ACT(sc[:, hq * 4:(hq + 1) * 4, :], ps[pb][:, :].rearrange("p (j t) -> p j t", j=4), AF.Copy, reads=[("ps", pb)], writes=[("sc", hq)])

            def routeA2(tt, bb, grp=None, part=None):
                xt, h2, gate, idxi = xt_[bb], h2_[bb], gate_[bb], idxi_[bb]
                KX, KH, KG, KI = ("xt", bb), ("h2", bb), ("gate", bb), ("idxi", bb)
                if part is None:
                    for hp0 in (range(0, 16, 4) if grp is None else [4 * grp]):
                        hps = list(range(hp0, hp0 + 4))
                        for hp in hps:
                            P.op("dve", lambda e, hp=hp: e.max(out=mx[:, hp, 0:8], in_=sc[:, hp, :]), reads=[("sc", hp // 4)], writes=[("mx", hp, 0)])
                        for hp in hps:
                            P.op("dve", lambda e, hp=hp: e.max_index(out=mi[:, hp, 0:8], in_max=mx[:, hp, 0:8], in_values=sc[:, hp, :]),
                                 reads=[("sc", hp // 4), ("mx", hp, 0)], writes=[("mi", hp, 0)])
                        for hp in hps:
                            P.op("dve", lambda e, hp=hp: e.match_replace(out=wk4[hp % 4][:, 0:128], in_to_replace=mx[:, hp, 0:8], in_values=sc[:, hp, :], imm_value=-1e30),
                                 reads=[("sc", hp // 4), ("mx", hp, 0)], writes=[("wk", hp % 4)])
                        for hp in hps:
                            P.op("dve", lambda e, hp=hp: e.max(out=mx[:, hp, 8:16], in_=wk4[hp % 4][:, 0:128]), reads=[("wk", hp % 4)], writes=[("mx", hp, 1)])
                        for hp in hps:
                            P.op("dve", lambda e, hp=hp: e.max_index(out=mi[:, hp, 8:16], in_max=mx[:, hp, 8:16], in_values=wk4[hp % 4][:, 0:128]),
                                 reads=[("wk", hp % 4), ("mx", hp, 1)], writes=[("mi", hp, 1)])
                if part == 1 or (part is None and grp is None):
                    CP(mif[:], mi[:], reads=["mi"], writes=["mif"])
                    mxv = mx[:].rearrange("p (h two) k -> p h two k", two=2)
                    TT(cand[:].rearrange("p h (a b) -> p h a b", a=16),
                       mxv[:, :, 0, :].unsqueeze(3).to_broadcast([128, 8, 16, 16]),
                       mxv[:, :, 1, :].unsqueeze(2).to_broadcast([128, 8, 16, 16]), ALU.add, reads=["mx"], writes=["cand"])
                    for h0 in range(0, 8, 4):
                        hs = list(range(h0, h0 + 4))
                        for h in hs:
                            P.op("dve", lambda e, h=h: e.max(out=best[:, h, 0:8], in_=cand[:, h, :]), reads=["cand"], writes=[("best", h, 0)])
                        for h in hs:
                            P.op("dve", lambda e, h=h: e.max_index(out=pos[:, h, 0:8], in_max=best[:, h, 0:8], in_values=cand[:, h, :]),
                                 reads=["cand", ("best", h, 0)], writes=[("pos", h, 0)])
                        for h in hs:
                            P.op("dve", lambda e, h=h: e.match_replace(out=wk4[h % 4][:], in_to_replace=best[:, h, 0:8], in_values=cand[:, h, :], imm_value=-1e30),
                                 reads=["cand", ("best", h, 0)], writes=[("wk", h % 4)])
                        for h in hs:
                            P.op("dve", lambda e, h=h: e.max(out=best[:, h, 8:16], in_=wk4[h % 4][:]), reads=[("wk", h % 4)], writes=[("best", h, 1)])
                        for h in hs:
                            P.op("dve", lambda e, h=h: e.max_index(out=pos[:, h, 8:16], in_max=best[:, h, 8:16], in_values=wk4[h % 4][:]),
                                 reads=[("wk", h % 4), ("best", h, 1)], writes=[("pos", h, 1)])
                if part == 2 or (part is None and grp is None):
                    CP(posf[:], pos[:], reads=["pos"], writes=["posf"])
                    pfl = posf[:].rearrange("p h k -> p (h k)")
                    TS(ai[:], pfl, 1.0 / 16, -0.46875, ALU.mult, ALU.add, reads=["posf"], writes=["ai"])
                    CP(af[:], ai[:], reads=["ai"], writes=["af"])
                    STT(bf[:], af[:], -16.0, pfl, ALU.mult, ALU.add, reads=["af", "posf"], writes=["bf"])
                    mfv = mif[:].rearrange("p (h two) k -> p h two k", two=2)
                    io_b = iota16[:].unsqueeze(1).unsqueeze(1).to_broadcast([128, 8, 16, 16])
                    for (sel, src, which) in ((i0s, af, 0), (i1s, bf, 1)):
                        sv = src[:].rearrange("p (h k) -> p h k", h=8).unsqueeze(3).to_broadcast([128, 8, 16, 16])
                        TT(eq[:], sv, io_b, ALU.is_equal, reads=["af", "bf", "c_iota16"], writes=["eq"])
                        TT(eq[:], eq[:], mfv[:, :, which, :].unsqueeze(2).to_broadcast([128, 8, 16, 16]), ALU.mult, reads=["eq", "mif"], writes=["eq"])
                        P.op("dve", lambda e, sel=sel: e.reduce_sum(out=sel[:], in_=eq[:].rearrange("p h k a -> p (h k) a"), axis=AX.X),
                             reads=["eq"], writes=["i0s" if which == 0 else "i1s"])
                    STT(idxf[:], i0s[:], 128.0, i1s[:], ALU.mult, ALU.add, reads=["i0s", "i1s"], writes=["idxf"])
                    TS(idxf[:], idxf[:], 0.0, 16383.0, ALU.max, ALU.min, reads=["idxf"], writes=["idxf"])
                    TT(gate[:], best[:], best[:, :, 0:1].to_broadcast([128, 8, 16]), ALU.subtract, reads=["best"], writes=[KG])
                    ACT(gate[:], gate[:], AF.Exp, reads=[KG], writes=[KG])
                    P.op("dve", lambda e: e.reduce_sum(out=gs[:], in_=gate[:], axis=AX.X), reads=[KG], writes=["gs"])
                    RECIP(gs[:], gs[:], reads=["gs"], writes=["gs"])
                    TT(gate[:], gate[:], gs[:].unsqueeze(2).to_broadcast([128, 8, 16]), ALU.mult, reads=[KG, "gs"], writes=[KG])

                    CP(idxi[:], idxf[:], reads=["idxf"], writes=[KI])

            def expertB(tt, bb):
                xt, h2, gate, idxi = xt_[bb], h2_[bb], gate_[bb], idxi_[bb]
                KX, KH, KG, KI = ("xt", bb), ("h2", bb), ("gate", bb), ("idxi", bb)
                t0 = tt * 128
                gfl = gate[:].rearrange("p h k -> p (h k)")
                for sl_ in range(128):
                    b = sl_ % NB
                    P.dma("pool", lambda e, b=b, sl_=sl_: e.indirect_dma_start(
                        out=ur[b][:], out_offset=None, in_=peer_u[l],
                        in_offset=bass.IndirectOffsetOnAxis(ap=idxi[:, sl_:sl_ + 1], axis=0)),
                        reads=[KI], writes=[("ur", b)])
                    STT(junk2[:], ur[b][:], 1.0, h2[:], ALU.mult, ALU.mult, reads=[("ur", b), KH], writes=["junk2", ("AT", sl_)],
                        accum_out=AT[:, sl_:sl_ + 1])
                    if tt + 1 < NT and sl_ in (63, 87, 111):
                        routeA2(tt + 1, (tt + 1) % 2, grp=(sl_ - 63) // 24)
                if tt + 1 < NT:
                    routeA2(tt + 1, (tt + 1) % 2, grp=3)
                ACT(actT[:], AT[:], AF.Gelu_apprx_tanh, reads=["AT"], writes=["actT"])
                TT(actT[:], actT[:], gfl, ALU.mult, reads=["actT", KG], writes=["actT"])
                for sl_ in range(128):
                    b = sl_ % NB
                    P.dma("pool", lambda e, b=b, sl_=sl_: e.indirect_dma_start(
                        out=ur[b][:], out_offset=None, in_=peer_v[l],
                        in_offset=bass.IndirectOffsetOnAxis(ap=idxi[:, sl_:sl_ + 1], axis=0)),
                        reads=[KI], writes=[("ur", b)])
                    STT(xt[:], ur[b][:], actT[:, sl_:sl_ + 1], xt[:], ALU.mult, ALU.add, reads=[("ur", b), "actT", KX], writes=[KX])
                    if tt + 1 < NT and sl_ in (23, 63):
                        routeA2(tt + 1, (tt + 1) % 2, part=(1 if sl_ == 23 else 2))
                DMA("sp", xres[t0:t0 + 128, :], xt[:], reads=[KX], writes=[("xres", tt)])

            routeA(0, 0)
            routeA2(0, 0)
            for tt in range(NT):
                if tt + 1 < NT:
                    routeA(tt + 1, (tt + 1) % 2)
                expertB(tt, tt % 2)
            ph.done()

    if want("final"):
        ph = Ph()
        gbc = ph.t("gbc", [128, D]); eps_t = ph.t("eps_t", [128, 1])
        junk = ph.t("junk", [128, D]); ss = ph.t("ss", [128, 1]); rs = ph.t("rs", [128, 1])
        xt = [ph.t("fxt%d" % i, [128, D]) for i in range(2)]
        ht = [ph.t("fht%d" % i, [128, D]) for i in range(2)]
        MEMSET(eps_t[:], EPS, writes=["eps"])
        DMA("sp", gbc[:], gfin[0:1, :].partition_broadcast(128), writes=["gbc"])
        for tt in range(NT):
            b = tt % 2
            DMA("sp", xt[b][:], xres[tt * 128:(tt + 1) * 128, :], reads=["xres"], writes=[("xt", b)])
            rmsnorm_rows(ph, xt[b][:], gbc[:], ht[b][:], junk[:], ss, rs, eps_t, D, ("xt", b), ("ht", b))
            DMA("pool", y_out[tt * 128:(tt + 1) * 128, :], ht[b][:], reads=[("ht", b)], writes=["y"])
        ph.done()
    P.close()
    return nc


def prep_shared(inp):
    f = lambda a: np.ascontiguousarray(np.asarray(a, dtype=np.float32))
    sh = {}
    for k in ("norm1_g", "norm2_g", "w_in", "ssd_dt_bias", "ssd_a_log", "ssd_d", "ssd_norm_g", "s5_w_glu",
              "w_branch", "w_out", "peer_w_query"):
        sh[k] = f(inp[k])
    for i in range(L):
        sh["peer_u%d" % i] = f(np.asarray(inp["peer_u"])[i])
        sh["peer_v%d" % i] = f(np.asarray(inp["peer_v"])[i])
    sh["final_norm_g"] = f(inp["final_norm_g"]).reshape(1, D)
    sh["ssd_cw"] = f(np.transpose(np.asarray(inp["ssd_conv_w"]).reshape(L, 4, 10, 128), (0, 3, 2, 1)))
    sh["ssd_cb"] = f(np.transpose(np.asarray(inp["ssd_conv_b"]).reshape(L, 10, 128), (0, 2, 1)))
    sh["sc_cw"] = f(np.transpose(np.asarray(inp["sc_conv_w"]).reshape(L, 3, 4, 128), (0, 3, 2, 1)))
    sh["cf_cw"] = f(np.transpose(np.asarray(inp["cf_conv_w"]).reshape(L, 31, 4, 128), (0, 3, 2, 1)))
    sh["cf_g"] = f(np.transpose(np.asarray(inp["cf_ln_g"]).reshape(L, 4, 128), (0, 2, 1)))
    sh["cf_b"] = f(np.transpose(np.asarray(inp["cf_ln_b"]).reshape(L, 4, 128), (0, 2, 1)))
    sh["s5_dsk"] = f(np.transpose(np.asarray(inp["s5_d"]).reshape(L, 4, 128), (0, 2, 1)))
    for nm, src in (("s5_lre", "s5_lam_re"), ("s5_lim", "s5_lam_im")):
        sh[nm] = f(np.transpose(np.asarray(inp[src]).reshape(L, 16, 128), (0, 2, 1)))
    ls = np.repeat(np.asarray(inp["s5_log_step"])[:, :, None], 64, axis=2)
    sh["s5_lst"] = f(np.transpose(ls.reshape(L, 16, 128), (0, 2, 1)))
    Bre = np.zeros((L, 128, 16, 128), np.float32); Bim = np.zeros_like(Bre)
    Cre = np.zeros((L, 128, 16, 128), np.float32); Cim = np.zeros_like(Cre)
    b_re = np.asarray(inp["s5_b_re"]); b_im = np.asarray(inp["s5_b_im"])
    c_re = np.asarray(inp["s5_c_re"]); c_im = np.asarray(inp["s5_c_im"])
    for g in range(32):
        tp, gl = g // 2, g % 2
        r0 = (g % 8) * 16
        for dst, src in ((Bre, b_re), (Bim, b_im)):
            dst[:, r0:r0 + 16, tp, gl * 64:(gl + 1) * 64] = np.transpose(src[:, g], (0, 2, 1))
        for dst, src in ((Cre, c_re), (Cim, c_im)):
            dst[:, gl * 64:(gl + 1) * 64, tp, r0:r0 + 16] = np.transpose(src[:, g], (0, 2, 1))
    sh["s5_Bre"], sh["s5_Bim"], sh["s5_Cre"], sh["s5_Cim"] = Bre, Bim, Cre, Cim
    sh["skT"] = f(np.transpose(np.asarray(inp["peer_sub_keys"]), (0, 1, 3, 2)))
    sh["c_ident"] = np.eye(128, dtype=np.float32)
    sh["c_triu"] = np.triu(np.ones((128, 128), np.float32))
    sh["c_ones"] = np.ones((128, 128), np.float32)
    sh["c_iota_t"] = np.ascontiguousarray(np.broadcast_to(np.arange(S, dtype=np.float32)[None, :], (128, S)))
    sh["c_iota16"] = np.ascontiguousarray(np.broadcast_to(np.arange(16, dtype=np.float32)[None, :], (128, 16)))
    return sh


def kernel(**inputs):
    sh = prep_shared(inputs)
    x = np.asarray(inputs["x"], dtype=np.float32)
    nc = build()
    in_maps = []
    for c in range(8):
        m = dict(sh)
        m["x"] = np.ascontiguousarray(x[c])
        in_maps.append(m)
    res = run_bass_kernel_spmd(nc, in_maps, core_ids=list(range(8)))
    return np.stack([np.asarray(res.results[c]["y"]).reshape(S, D) for c in range(8)], axis=0).astype(np.float32)
```

```python
import numpy as np
from contextlib import ExitStack
import concourse.bass as bass
import concourse.mybir as mybir
from concourse.bass_utils import run_bass_kernel_spmd

F32 = mybir.dt.float32
I32 = mybir.dt.int32
U16 = mybir.dt.uint16
ALU = mybir.AluOpType
AF = mybir.ActivationFunctionType
AX = mybir.AxisListType

L = 2
D = 1024
S = 2048
NT = S // 128
INC = 9228
EPS = 1e-6
TWO_PI = float(2 * np.pi)

ENGS = ("pe", "act", "dve", "pool", "sp")
SEM_LIMIT = 24000
DMA_SLOTS = 16
SAME_ENG_BIG = 1 << 60


class Prog:
    def __init__(self, nc):
        self.nc = nc
        self.es = ExitStack()
        self.streams = {e: [] for e in ENGS}
        self.nsem = 0
        self.esem = {e: self._newsem() for e in ENGS}
        self.ecnt = {e: 0 for e in ENGS}
        self.known = {e: {} for e in ENGS}
        self.state = {}
        self.children = {}
        self.dslots = {q: [[self._newsem(), 0] for _ in range(DMA_SLOTS)] for q in ("sp", "pool", "act")}
        self.dnext = {q: 0 for q in ("sp", "pool", "act")}
        self.nops = 0
        self.evsize = {}

    def _newsem(self):
        self.nsem += 1
        return self.es.enter_context(self.nc.semaphore("s%d" % self.nsem))

    @staticmethod
    def _norm(k):
        return k if isinstance(k, tuple) else (k,)

    def _related(self, k):
        out = []
        for i in range(1, len(k) + 1):
            p = k[:i]
            if p in self.state:
                out.append(p)
        for c in self.children.get(k, ()):
            if c != k:
                out.append(c)
        return out

    def _touch(self, k):
        if k not in self.state:
            self.state[k] = [None, {}]
            for i in range(1, len(k)):
                self.children.setdefault(k[:i], set()).add(k)
        return self.state[k]

    def _deps(self, reads, writes):
        deps = []
        for k in reads:
            for r in self._related(self._norm(k)):
                w = self.state[r][0]
                if w is not None:
                    deps.append(w)
        for k in writes:
            for r in self._related(self._norm(k)):
                st = self.state[r]
                if st[0] is not None:
                    deps.append(st[0])
                deps.extend(st[1].values())
        return deps

    def _record(self, eng, ev, reads, writes):
        for k in reads:
            st = self._touch(self._norm(k))
            st[1][(eng, id(ev[0]))] = ev
        for k in writes:
            st = self._touch(self._norm(k))
            st[0] = ev
            st[1] = {}

    def _waits(self, eng, deps):
        need = {}
        for (s, v) in deps:
            if s is self.esem[eng]:
                if eng in ("pe", "sp") or self.evsize.get((id(s), v), 0) >= SAME_ENG_BIG:
                    continue
            if self.known[eng].get(id(s), 0) >= v:
                continue
            if id(s) not in need or need[id(s)][1] < v:
                need[id(s)] = (s, v)
        for (s, v) in need.values():
            self.known[eng][id(s)] = v
        return list(need.values())

    def op(self, eng, fn, reads=(), writes=(), n=0):
        psk = [k[:2] for k in list(reads) + list(writes) if isinstance(k, tuple) and k[0] == "ps"]
        if psk:
            reads = [k for k in reads if not (isinstance(k, tuple) and k[0] == "ps")]
            writes = [k for k in writes if not (isinstance(k, tuple) and k[0] == "ps")] + list(dict.fromkeys(psk))
        deps = self._deps(reads, writes)
        waits = self._waits(eng, deps)
        if self.ecnt[eng] >= SEM_LIMIT:
            self.esem[eng] = self._newsem()
            self.ecnt[eng] = 0
        self.ecnt[eng] += 1
        ev = (self.esem[eng], self.ecnt[eng])
        self.evsize[(id(ev[0]), ev[1])] = n
        self.streams[eng].append((waits, fn, ev[0], 1))
        self._record(eng, ev, reads, writes)
        self.nops += 1
        return ev

    def dma(self, q, fn, reads=(), writes=()):
        deps = self._deps(reads, writes)
        i = self.dnext[q]
        self.dnext[q] += 1
        slot = self.dslots[q][i % DMA_SLOTS]
        if slot[1] > 0:
            deps.append((slot[0], slot[1]))
        w = self._waits(q, deps)
        if slot[1] + 16 > SEM_LIMIT:
            slot[0] = self._newsem()
            slot[1] = 0
        slot[1] += 16
        ev = (slot[0], slot[1])
        self.streams[q].append((w, fn, ev[0], 16))
        self._record(q, ev, reads, writes)
        self.nops += 1
        return ev

    def barrier(self):
        evs = []
        for e in ENGS:
            if self.ecnt[e] > 0:
                evs.append((self.esem[e], self.ecnt[e]))
        for q in self.dslots:
            for s, v in self.dslots[q]:
                if v > 0:
                    evs.append((s, v))
        for e in ENGS:
            w = self._waits(e, evs)
            if w:
                self.streams[e].append((w, None, None, 0))
        self.state = {}
        self.children = {}

    def flush(self):
        self.barrier()
        nc = self.nc
        streams = self.streams
        self.streams = {e: [] for e in ENGS}
        with nc.Block() as block:
            def mk(ename):
                def body(eng):
                    for (waits, fn, sem, inc) in streams[ename]:
                        for (s, v) in waits:
                            eng.wait_ge(s, v)
                        if fn is not None:
                            fn(eng).then_inc(sem, inc)
                return body
            block.tensor(mk("pe"))
            block.scalar(mk("act"))
            block.vector(mk("dve"))
            block.gpsimd(mk("pool"))
            block.sync(mk("sp"))

    def close(self):
        self.es.close()


R_XBC, R_S5, R_SC, R_CF, R_GATE, R_END = 0, 1280, 1792, 3328, 4352, 8448
OFF_XBC, OFF_DT, OFF_S5 = 768, 2048, 2060
YA, YB, YC, YD = 0, 768, 1280, 1792


def wcol(r):
    return OFF_XBC + r if r < 1280 else OFF_S5 + (r - 1280)


def build(dbg=False, phases=None, nlayers=L):
    nc = bass.Bass("TRN2", target_bir_lowering=False)
    P = Prog(nc)
    kin = "ExternalInput"

    def din(name, shape, dt=F32):
        return nc.dram_tensor(name, list(shape), dt, kind=kin).ap()

    def dscr(name, shape, dt=F32):
        return nc.dram_tensor(name, list(shape), dt, kind=("ExternalOutput" if dbg else "Internal")).ap()

    x_in = din("x", [S, D])
    g1 = din("norm1_g", [L, D]); g2 = din("norm2_g", [L, D]); gfin = din("final_norm_g", [1, D])
    w_in = din("w_in", [L, D, INC])
    ssd_cw = din("ssd_cw", [L, 128, 10, 4]); ssd_cb = din("ssd_cb", [L, 128, 10])
    ssd_dtb = din("ssd_dt_bias", [L, 12]); ssd_alog = din("ssd_a_log", [L, 12]); ssd_dsk = din("ssd_d", [L, 12])
    ssd_ng = din("ssd_norm_g", [L, 768])
    s5_lre = din("s5_lre", [L, 128, 16]); s5_lim = din("s5_lim", [L, 128, 16]); s5_lst = din("s5_lst", [L, 128, 16])
    s5_Bre = din("s5_Bre", [L, 128, 16, 128]); s5_Bim = din("s5_Bim", [L, 128, 16, 128])
    s5_Cre = din("s5_Cre", [L, 128, 16, 128]); s5_Cim = din("s5_Cim", [L, 128, 16, 128])
    s5_dsk = din("s5_dsk", [L, 128, 4]); s5_wglu = din("s5_w_glu", [L, 512, 512])
    sc_cw = din("sc_cw", [L, 128, 4, 3])
    cf_cw = din("cf_cw", [L, 128, 4, 31]); cf_g = din("cf_g", [L, 128, 4]); cf_b = din("cf_b", [L, 128, 4])
    w_branch = din("w_branch", [L, 2304, D]); w_out = din("w_out", [L, D, D])
    wq = din("peer_w_query", [L, D, 2048]); skT = din("skT", [L, 2, 128, 128])
    peer_u = [din("peer_u%d" % i, [16384, D]) for i in range(L)]
    peer_v = [din("peer_v%d" % i, [16384, D]) for i in range(L)]
    c_ident = din("c_ident", [128, 128]); c_triu = din("c_triu", [128, 128]); c_ones = din("c_ones", [128, 128])
    c_iota_t = din("c_iota_t", [128, S]); c_iota16 = din("c_iota16", [128, 16])

    y_out = nc.dram_tensor("y", [S, D], F32, kind="ExternalOutput").ap()
    xres = dscr("xres", [S, D])
    projT = dscr("projT", [R_END, S])
    z_tm = dscr("z_tm", [S, 768]); dt_tm = dscr("dt_tm", [S, 12])
    yT = dscr("yT", [2304, S])
    hTd = dscr("hTd", [D, S])

    ps = [nc.alloc_psum_tensor("ps%d" % i, [128, 512], F32) for i in range(8)]

    def DMA(q, out, in_, reads=(), writes=()):
        P.dma(q, lambda e: e.dma_start(out=out, in_=in_), reads, writes)

    def nel(ap, kw=None):
        if kw and kw.get("accum_out") is not None:
            return 0
        return int(np.prod(ap.shape[1:]))

    def ACT(out, in_, func, reads=(), writes=(), **kw):
        P.op("act", lambda e: e.activation(out=out, in_=in_, func=func, **kw), reads, writes, n=nel(out, kw))

    def TT(out, in0, in1, op, reads=(), writes=(), eng="dve"):
        P.op(eng, lambda e: e.tensor_tensor(out=out, in0=in0, in1=in1, op=op), reads, writes, n=nel(out))

    def TS(out, in0, s1, s2, op0, op1=None, reads=(), writes=(), eng="dve", **kw):
        if op1 is None:
            P.op(eng, lambda e: e.tensor_scalar(out=out, in0=in0, scalar1=s1, scalar2=None, op0=op0, **kw), reads, writes, n=nel(out, kw))
        else:
            P.op(eng, lambda e: e.tensor_scalar(out=out, in0=in0, scalar1=s1, scalar2=s2, op0=op0, op1=op1, **kw), reads, writes, n=nel(out, kw))

    def STT(out, in0, sc, in1, op0, op1, reads=(), writes=(), **kw):
        P.op("dve", lambda e: e.scalar_tensor_tensor(out=out, in0=in0, scalar=sc, in1=in1, op0=op0, op1=op1, **kw), reads, writes, n=nel(out, kw))

    def CP(out, in_, reads=(), writes=(), eng="dve"):
        P.op(eng, lambda e: e.tensor_copy(out=out, in_=in_), reads, writes, n=nel(out))

    def MM(out, lhsT, rhs, start, stop, reads=(), writes=()):
        P.op("pe", lambda e: e.matmul(out, lhsT=lhsT, rhs=rhs, start=start, stop=stop), reads, writes)

    def TR(out, in_, ident, reads=(), writes=()):
        P.op("pe", lambda e: e.transpose(out=out, in_=in_, identity=ident), reads, writes)

    def MEMSET(ap, v, writes=(), eng="dve"):
        P.op(eng, lambda e: e.memset(ap, v), (), writes, n=nel(ap))

    def DUMP(name, ap, shape, reads, dt=F32):
        if not dbg:
            return
        o = nc.dram_tensor("dbg_" + name, list(shape), dt, kind="ExternalOutput").ap()
        DMA("sp", o, ap, reads=reads)

    def RECIP(out, in_, reads=(), writes=()):
        P.op("dve", lambda e: e.reciprocal(out=out, in_=in_), reads, writes, n=nel(out))

    class Ph:
        uid = 0
        def __init__(self):
            self.es = ExitStack()
            self.n = 0
        def t(self, name, shape, dt=F32):
            Ph.uid += 1
            return self.es.enter_context(nc.sbuf_tensor("%s_%d" % (name, Ph.uid), list(shape), dt))
        def done(self):
            P.flush()
            self.es.close()

    def want(p):
        return phases is None or p in phases

    def load_consts(ph, names):
        out = {}
        for n in names:
            src = {"ident": c_ident, "triu": c_triu, "ones": c_ones, "iota_t": c_iota_t, "iota16": c_iota16}[n]
            t = ph.t("c_" + n, src.shape)
            DMA("sp", t[:], src[:, :], writes=["c_" + n])
            out[n] = t
        return out

    def rmsnorm_rows(ph, xt, gbc, ht, junk, ss, rs, eps_t, n, kx, kh):
        ACT(junk, xt, AF.Square, reads=[kx], writes=["junk", "ss"], accum_out=ss[:, 0:1])
        ACT(rs[:, 0:1], ss[:, 0:1], AF.Sqrt, reads=["ss", "eps"], writes=["rs"], bias=eps_t[:, 0:1], scale=1.0 / n)
        RECIP(rs[:, 0:1], rs[:, 0:1], reads=["rs"], writes=["rs"])
        STT(ht, xt, rs[:, 0:1], gbc, ALU.mult, ALU.mult, reads=[kx, "rs", "gbc"], writes=[kh])

    for l in range(nlayers):
        xsrc = x_in if l == 0 else xres

        if want("proj"):
            ph = Ph()
            C = load_consts(ph, ["ident"])
            hT = ph.t("hT", [128, 8, S])
            gbc = ph.t("gbc", [128, D])
            eps_t = ph.t("eps_t", [128, 1])
            junk = ph.t("junk", [128, D])
            ss = ph.t("ss", [128, 1]); rs = ph.t("rs", [128, 1])
            xt = [ph.t("xt%d" % i, [128, D]) for i in range(2)]
            ht = [ph.t("ht%d" % i, [128, D]) for i in range(2)]
            wt = [ph.t("wt%d" % i, [128, 8, 128]) for i in range(3)]
            ot = [ph.t("ot%d" % i, [128, S]) for i in range(2)]
            wz = ph.t("wz", [128, 8, 768]); wdt = ph.t("wdt", [128, 8, 12])
            zo = [ph.t("zo%d" % i, [128, 780]) for i in range(2)]
            MEMSET(eps_t[:], EPS, writes=["eps"])
            DMA("sp", gbc[:], g1[l:l + 1, :].partition_broadcast(128), writes=["gbc"])
            DMA("sp", wz[:], w_in[l, :, 0:768].rearrange("(k p) c -> p k c", p=128), writes=["wz"])
            DMA("sp", wdt[:], w_in[l, :, OFF_DT:OFF_DT + 12].rearrange("(k p) c -> p k c", p=128), writes=["wdt"])
            for tt in range(NT):
                b = tt % 2
                DMA("sp", xt[b][:], xsrc[tt * 128:(tt + 1) * 128, :], reads=["xres"], writes=[("xt", b)])
                rmsnorm_rows(ph, xt[b][:], gbc[:], ht[b][:], junk[:], ss, rs, eps_t, D, ("xt", b), ("ht", b))
                for half in range(2):
                    for k in range(4):
                        kk = half * 4 + k
                        TR(ps[half][:, k * 128:(k + 1) * 128], ht[b][:, kk * 128:(kk + 1) * 128], C["ident"][:],
                           reads=[("ht", b), "c_ident"], writes=[("ps", half)])
                    dst = hT[:, half * 4:(half + 1) * 4, tt * 128:(tt + 1) * 128]
                    src = ps[half][:, :].rearrange("p (k t) -> p k t", k=4)
                    if half == 0:
                        ACT(dst, src, AF.Copy, reads=[("ps", half)], writes=[("hT", tt)])
                    else:
                        CP(dst, src, reads=[("ps", half)], writes=[("hT", tt)])
                DMA("pool", hTd[:, tt * 128:(tt + 1) * 128].rearrange("(k p) t -> p k t", p=128), hT[:, :, tt * 128:(tt + 1) * 128],
                    reads=[("hT", tt)], writes=["hTd"])
                zb = zo[tt % 2]
                for (c0, c1, pb) in ((0, 512, 2), (512, 768, 3)):
                    for k in range(8):
                        MM(ps[pb][:, 0:c1 - c0], hT[:, k, tt * 128:(tt + 1) * 128], wz[:, k, c0:c1], k == 0, k == 7,
                           reads=[("hT", tt), "wz"], writes=[("ps", pb)])
                    ACT(zb[:, c0:c1], ps[pb][:, 0:c1 - c0], AF.Copy, reads=[("ps", pb)], writes=[("zo", tt % 2, c0)])
                for k in range(8):
                    MM(ps[3][:, 256:268], hT[:, k, tt * 128:(tt + 1) * 128], wdt[:, k, :], k == 0, k == 7,
                       reads=[("hT", tt), "wdt"], writes=[("ps", 3, "dt")])
                CP(zb[:, 768:780], ps[3][:, 256:268], reads=[("ps", 3, "dt")], writes=[("zo", tt % 2, 768)])
                DMA("pool", z_tm[tt * 128:(tt + 1) * 128, :], zb[:, 0:768], reads=[("zo", tt % 2)], writes=["z_tm"])
                DMA("pool", dt_tm[tt * 128:(tt + 1) * 128, :], zb[:, 768:780], reads=[("zo", tt % 2)], writes=["dt_tm"])
            nct = R_GATE // 128
            for ct in range(nct):
                wb_ = ct % 3
                c0 = wcol(ct * 128)
                DMA("sp", wt[wb_][:], w_in[l, :, c0:c0 + 128].rearrange("(k p) c -> p k c", p=128), writes=[("wt", wb_)])
                ob = ct % 2
                for tc in range(4):
                    pb = 4 + (ct * 4 + tc) % 4
                    for k in range(8):
                        MM(ps[pb][:, :], wt[wb_][:, k, :], hT[:, k, tc * 512:(tc + 1) * 512], k == 0, k == 7,
                           reads=[("wt", wb_), "hT"], writes=[("ps", pb)])
                    if tc % 2 == 0:
                        ACT(ot[ob][:, tc * 512:(tc + 1) * 512], ps[pb][:, :], AF.Copy, reads=[("ps", pb)], writes=[("ot", ob, tc)])
                    else:
                        CP(ot[ob][:, tc * 512:(tc + 1) * 512], ps[pb][:, :], reads=[("ps", pb)], writes=[("ot", ob, tc)])
                DMA("pool", projT[ct * 128:(ct + 1) * 128, :], ot[ob][:], reads=[("ot", ob)], writes=["projT"])
            ph.done()

        if want("s5"):
            ph = Ph()
            C = load_consts(ph, ["iota_t"])
            iota = C["iota_t"]
            lre = ph.t("lre", [128, 16]); lim = ph.t("lim", [128, 16]); lst = ph.t("lst", [128, 16])
            DMA("sp", lre[:], s5_lre[l], writes=["lre"])
            DMA("sp", lim[:], s5_lim[l], writes=["lim"])
            DMA("sp", lst[:], s5_lst[l], writes=["lst"])
            Bre = ph.t("Bre", [128, 16, 128]); Bim = ph.t("Bim", [128, 16, 128])
            Cre = ph.t("Cre", [128, 16, 128]); Cim = ph.t("Cim", [128, 16, 128])
            Ctr = ph.t("Ctr", [128, 16, 128]); Cti = ph.t("Cti", [128, 16, 128])
            DMA("sp", Bre[:], s5_Bre[l], writes=["Bre"]); DMA("sp", Bim[:], s5_Bim[l], writes=["Bim"])
            DMA("sp", Cre[:], s5_Cre[l], writes=["Cre"]); DMA("sp", Cim[:], s5_Cim[l], writes=["Cim"])
            dsk = ph.t("s5dsk", [128, 4]); wgl = ph.t("wgl", [128, 4, 512])
            DMA("sp", dsk[:], s5_dsk[l], writes=["dsk"])
            DMA("sp", wgl[:], s5_wglu[l].rearrange("(k p) c -> p k c", p=128), writes=["wgl"])
            sm = {n: ph.t("s5_" + n, [128, 16]) for n in
                  ("dl", "are", "th", "r", "k", "phi", "sn", "cs", "th2", "nr", "ni", "den", "kre", "kim", "t1", "t2")}
            K_ = "s5small"
            ACT(sm["dl"][:], lst[:], AF.Exp, reads=["lst"], writes=[K_])
            TT(sm["are"][:], lre[:], sm["dl"][:], ALU.mult, reads=["lre", K_], writes=[K_])
            TT(sm["th"][:], lim[:], sm["dl"][:], ALU.mult, reads=["lim", K_], writes=[K_])
            ACT(sm["r"][:], sm["are"][:], AF.Exp, reads=[K_], writes=[K_])

            TWO_PI_LO = 6.283185
            MAGIC = 12582912.0
            halfpi = ph.t("s5_halfpi", [128, 1])
            MEMSET(halfpi[:], 1.5707963, writes=["halfpi"])

            def sincos_small(th, out_s, out_c):
                TS(sm["k"][:], th, 1.0 / TWO_PI, None, ALU.mult, reads=[K_], writes=[K_])
                TS(sm["phi"][:], sm["k"][:], MAGIC, None, ALU.add, reads=[K_], writes=[K_])
                STT(sm["phi"][:], sm["phi"][:], MAGIC, sm["k"][:], ALU.subtract, ALU.subtract, reads=[K_], writes=[K_])
                ACT(out_s, sm["phi"][:], AF.Sin, reads=[K_], writes=[K_], scale=-TWO_PI_LO)
                STT(sm["th2"][:], sm["phi"][:], -1.0, sm["phi"][:], ALU.mult, ALU.max, reads=[K_], writes=[K_])
                ACT(out_c, sm["th2"][:], AF.Sin, reads=[K_, "halfpi"], writes=[K_], scale=-TWO_PI_LO, bias=halfpi[:, 0:1])

            sincos_small(sm["th"][:], sm["sn"][:], sm["cs"][:])
            TS(sm["t2"][:], sm["th"][:], 1.0 / TWO_PI, None, ALU.mult, reads=[K_], writes=[K_])
            th2pi = ph.t("s5_th2pi", [128, 16])
            CP(th2pi[:], sm["t2"][:], reads=[K_], writes=["th2pi"])
            TT(sm["nr"][:], sm["r"][:], sm["cs"][:], ALU.mult, reads=[K_], writes=[K_])
            TS(sm["nr"][:], sm["nr"][:], -1.0, None, ALU.add, reads=[K_], writes=[K_])
            TT(sm["ni"][:], sm["r"][:], sm["sn"][:], ALU.mult, reads=[K_], writes=[K_])
            TT(sm["den"][:], lre[:], lre[:], ALU.mult, reads=["lre", K_], writes=[K_])
            TT(sm["t1"][:], lim[:], lim[:], ALU.mult, reads=["lim", K_], writes=[K_])
            TT(sm["den"][:], sm["den"][:], sm["t1"][:], ALU.add, reads=[K_], writes=[K_])
            RECIP(sm["den"][:], sm["den"][:], reads=[K_], writes=[K_])
            TT(sm["t1"][:], sm["nr"][:], lre[:], ALU.mult, reads=[K_, "lre"], writes=[K_])
            TT(sm["t2"][:], sm["ni"][:], lim[:], ALU.mult, reads=[K_, "lim"], writes=[K_])
            TT(sm["t1"][:], sm["t1"][:], sm["t2"][:], ALU.add, reads=[K_], writes=[K_])
            TT(sm["kre"][:], sm["t1"][:], sm["den"][:], ALU.mult, reads=[K_], writes=[K_])
            TT(sm["t1"][:], sm["ni"][:], lre[:], ALU.mult, reads=[K_, "lre"], writes=[K_])
            TT(sm["t2"][:], sm["nr"][:], lim[:], ALU.mult, reads=[K_, "lim"], writes=[K_])
            TT(sm["t1"][:], sm["t1"][:], sm["t2"][:], ALU.subtract, reads=[K_], writes=[K_])
            TT(sm["kim"][:], sm["t1"][:], sm["den"][:], ALU.mult, reads=[K_], writes=[K_])
            ctmp = ph.t("ctmp", [128, 128])
            for tp in range(16):
                TS(Ctr[:, tp, :], Cre[:, tp, :], sm["kre"][:, tp:tp + 1], None, ALU.mult, reads=["Cre", K_], writes=[("Ctr", tp)])
                TS(ctmp[:], Cim[:, tp, :], sm["kim"][:, tp:tp + 1], None, ALU.mult, reads=["Cim", K_], writes=["ctmp"])
                TT(Ctr[:, tp, :], Ctr[:, tp, :], ctmp[:], ALU.subtract, reads=[("Ctr", tp), "ctmp"], writes=[("Ctr", tp)])
                TS(Cti[:, tp, :], Cre[:, tp, :], sm["kim"][:, tp:tp + 1], -1.0, ALU.mult, ALU.mult, reads=["Cre", K_], writes=[("Cti", tp)])
                TS(ctmp[:], Cim[:, tp, :], sm["kre"][:, tp:tp + 1], None, ALU.mult, reads=["Cim", K_], writes=["ctmp"])
                TT(Cti[:, tp, :], Cti[:, tp, :], ctmp[:], ALU.subtract, reads=[("Cti", tp), "ctmp"], writes=[("Cti", tp)])

            for n_ in ("th", "r", "kre", "kim", "sn", "cs"):
                DUMP("sm_" + n_, sm[n_][:], [128, 16], [K_])
            uT = ph.t("uT", [128, S])
            cs = ph.t("s5cos", [128, S]); sn = ph.t("s5sin", [128, S])
            x1 = ph.t("s5x1", [128, S]); kf = ph.t("s5kf", [128, S])
            vre = ph.t("vre", [128, S]); vim = ph.t("vim", [128, S])
            gre = ph.t("gre", [128, S]); gim = ph.t("gim", [128, S])
            yg = ph.t("yg", [128, 4, S])
            t1 = [ph.t("s5t1_%d" % i, [128, 512]) for i in range(2)]
            t2 = [ph.t("s5t2_%d" % i, [128, 512]) for i in range(2)]

            def sincos_big(tp):
                TS(x1[:], iota[:], th2pi[:, tp:tp + 1], None, ALU.mult, reads=["c_iota_t", "th2pi"], writes=["x1"])
                TS(kf[:], x1[:], MAGIC, None, ALU.add, reads=["x1"], writes=["kf"])
                STT(kf[:], kf[:], MAGIC, x1[:], ALU.subtract, ALU.subtract, reads=["kf", "x1"], writes=["kf"])
                ACT(sn[:], kf[:], AF.Sin, reads=["kf"], writes=["sn"], scale=-TWO_PI_LO)
                STT(x1[:], kf[:], -1.0, kf[:], ALU.mult, ALU.max, reads=["kf"], writes=["x1"])
                ACT(cs[:], x1[:], AF.Sin, reads=["x1", "halfpi"], writes=["cs"], scale=-TWO_PI_LO, bias=halfpi[:, 0:1])

            for ct in range(4):
                DMA("sp", uT[:], projT[R_S5 + ct * 128:R_S5 + (ct + 1) * 128, :], reads=["projT"], writes=["uT"])
                for tpi in range(4):
                    tp = ct * 4 + tpi
                    sincos_big(tp)
                    for tc in range(4):
                        sl = slice(tc * 512, (tc + 1) * 512)
                        pa, pb = ps[4 + (tc % 2) * 2], ps[5 + (tc % 2) * 2]
                        ka, kb = ("ps", 4 + (tc % 2) * 2), ("ps", 5 + (tc % 2) * 2)
                        MM(pa[:, :], Bre[:, tp, :], uT[:, sl], True, True, reads=["Bre", "uT"], writes=[ka])
                        MM(pb[:, :], Bim[:, tp, :], uT[:, sl], True, True, reads=["Bim", "uT"], writes=[kb])
                        a, b2 = t1[tc % 2], t2[tc % 2]
                        TT(a[:], pa[:, :], cs[:, sl], ALU.mult, reads=[ka, "cs"], writes=[("t1", tc % 2)])
                        TT(b2[:], pb[:, :], sn[:, sl], ALU.mult, reads=[kb, "sn"], writes=[("t2", tc % 2)])
                        TT(vre[:, sl], a[:], b2[:], ALU.add, reads=[("t1", tc % 2), ("t2", tc % 2)], writes=[("vre", tc)])
                        TT(a[:], pb[:, :], cs[:, sl], ALU.mult, reads=[kb, "cs"], writes=[("t1", tc % 2)])
                        TT(b2[:], pa[:, :], sn[:, sl], ALU.mult, reads=[ka, "sn"], writes=[("t2", tc % 2)])
                        TT(vim[:, sl], a[:], b2[:], ALU.subtract, reads=[("t1", tc % 2), ("t2", tc % 2)], writes=[("vim", tc)])
                    if tp == 0:
                        DUMP("sn", sn[:], [128, S], ["sn"]); DUMP("cs", cs[:], [128, S], ["cs"])
                        DUMP("vre", vre[:], [128, S], ["vre"]); DUMP("vim", vim[:], [128, S], ["vim"])
                    rb = sm["r"][:, tp:tp + 1].to_broadcast([128, S])
                    P.op("dve", lambda e, rb=rb: e.tensor_tensor_scan(out=gre[:], data0=rb, data1=vre[:], initial=0.0, op0=ALU.mult, op1=ALU.add),
                         reads=["vre", K_], writes=["gre"])
                    P.op("dve", lambda e, rb=rb: e.tensor_tensor_scan(out=gim[:], data0=rb, data1=vim[:], initial=0.0, op0=ALU.mult, op1=ALU.add),
                         reads=["vim", K_], writes=["gim"])
                    if tp == 0:
                        DUMP("gre", gre[:], [128, S], ["gre"]); DUMP("gim", gim[:], [128, S], ["gim"])
                    TT(vre[:], cs[:], gre[:], ALU.mult, reads=["cs", "gre"], writes=["vre"])
                    TT(x1[:], sn[:], gim[:], ALU.mult, reads=["sn", "gim"], writes=["x1"])
                    TT(vre[:], vre[:], x1[:], ALU.subtract, reads=["vre", "x1"], writes=["vre"])
                    TT(vim[:], sn[:], gre[:], ALU.mult, reads=["sn", "gre"], writes=["vim"])
                    TT(x1[:], cs[:], gim[:], ALU.mult, reads=["cs", "gim"], writes=["x1"])
                    TT(vim[:], vim[:], x1[:], ALU.add, reads=["vim", "x1"], writes=["vim"])
                    for tc in range(4):
                        sl = slice(tc * 512, (tc + 1) * 512)
                        MM(ps[tc][:, :], Ctr[:, tp, :], vre[:, sl], tpi == 0, False, reads=[("Ctr", tp), "vre"], writes=[("ps", tc)])
                        MM(ps[tc][:, :], Cti[:, tp, :], vim[:, sl], False, tpi == 3, reads=[("Cti", tp), "vim"], writes=[("ps", tc)])
                for tc in range(4):
                    sl = slice(tc * 512, (tc + 1) * 512)
                    STT(yg[:, ct, sl], uT[:, sl], dsk[:, ct:ct + 1], ps[tc][:, :], ALU.mult, ALU.add,
                        reads=["uT", "dsk", ("ps", tc)], writes=[("yg", ct, tc)])
                if ct == 0:
                    DUMP("ypre", yg[:, 0, :], [128, S], [("yg", 0)])
                ACT(yg[:, ct, :], yg[:, ct, :], AF.Gelu_apprx_tanh, reads=[("yg", ct)], writes=[("yg", ct)])
            for co in range(4):
                for tc in range(4):
                    sl = slice(tc * 512, (tc + 1) * 512)
                    pb = 4 + (co * 4 + tc) % 4
                    for ci in range(4):
                        MM(ps[pb][:, :], wgl[:, ci, co * 128:(co + 1) * 128], yg[:, ci, sl], ci == 0, ci == 3,
                           reads=["wgl", "yg"], writes=[("ps", pb)])
                    tb = (co * 4 + tc) % 2
                    ACT(t1[tb][:], ps[pb][:, :], AF.Sigmoid, reads=[("ps", pb)], writes=[("t1", tb)])
                    TT(t1[tb][:], t1[tb][:], yg[:, co, sl], ALU.mult, reads=[("t1", tb), "yg"], writes=[("t1", tb)])
                    DMA("pool", yT[YB + co * 128:YB + (co + 1) * 128, sl], t1[tb][:], reads=[("t1", tb)], writes=["yT"])
            ph.done()

        gph = None
        gen = iter(())
        if want("sc") or want("cf") or want("ssd"):
            gph = Ph()
            ghT = [gph.t("ghT%d" % i, [128, 8, 512]) for i in range(2)]
            gwt = [gph.t("gwt%d" % i, [128, 8, 128]) for i in range(3)]
            gog = [gph.t("gog%d" % i, [128, 512]) for i in range(3)]

            def gate_groups():
                g = 0
                for tc in range(4):
                    hb = tc % 2
                    DMA("pool", ghT[hb][:], hTd[:, tc * 512:(tc + 1) * 512].rearrange("(k p) t -> p k t", p=128),
                        reads=["hTd"], writes=[("ghT", hb)])
                    for ct in range(R_GATE // 128, R_END // 128):
                        wb_ = g % 3
                        pb = 6 + g % 2
                        c0 = wcol(ct * 128)
                        DMA("pool", gwt[wb_][:], w_in[l, :, c0:c0 + 128].rearrange("(k p) c -> p k c", p=128), writes=[("gwt", wb_)])
                        for k in range(8):
                            MM(ps[pb][:, :], gwt[wb_][:, k, :], ghT[hb][:, k, :], k == 0, k == 7,
                               reads=[("gwt", wb_), ("ghT", hb)], writes=[("ps", pb)])
                        ACT(gog[wb_][:], ps[pb][:, :], AF.Copy, reads=[("ps", pb)], writes=[("gog", wb_)])
                        DMA("act", projT[ct * 128:(ct + 1) * 128, tc * 512:(tc + 1) * 512], gog[wb_][:], reads=[("gog", wb_)], writes=["projTg"])
                        g += 1
                        yield
            gen = gate_groups()

        def pump(n=1):
            for _ in range(n):
                next(gen, None)

        if want("sc"):
            ph = Ph()
            cw = ph.t("sccw", [128, 4, 3])
            DMA("sp", cw[:], sc_cw[l], writes=["cw"])
            bt = [ph.t("scb%d" % i, [128, S]) for i in range(2)]
            ctt = [ph.t("scc%d" % i, [128, S]) for i in range(2)]
            htt = [ph.t("sch%d" % i, [128, S]) for i in range(2)]
            chp = [ph.t("chp%d" % i, [128, S + 2]) for i in range(2)]
            acc = [ph.t("sca%d" % i, [128, S]) for i in range(2)]
            for i in range(2):
                MEMSET(chp[i][:, 0:2], 0.0, writes=[("chp", i, "h")])
            for ct in range(4):
                b = ct % 2
                r0 = R_SC + ct * 128
                DMA("sp", bt[b][:], projT[r0:r0 + 128, :], reads=["projT"], writes=[("bt", b)])
                DMA("sp", ctt[b][:], projT[r0 + 512:r0 + 640, :], reads=["projT"], writes=[("ct", b)])
                DMA("sp", htt[b][:], projT[r0 + 1024:r0 + 1152, :], reads=["projT"], writes=[("ht", b)])
                pump(3)
                TT(chp[b][:, 2:S + 2], ctt[b][:], htt[b][:], ALU.mult, reads=[("ct", b), ("ht", b)], writes=[("chp", b, "d")])
                TS(acc[b][:], chp[b][:, 0:S], cw[:, ct, 0:1], None, ALU.mult, reads=[("chp", b), "cw"], writes=[("acc", b)])
                for k in (1, 2):
                    STT(acc[b][:], chp[b][:, k:k + S], cw[:, ct, k:k + 1], acc[b][:], ALU.mult, ALU.add,
                        reads=[("chp", b), "cw", ("acc", b)], writes=[("acc", b)])
                TT(acc[b][:], acc[b][:], bt[b][:], ALU.mult, reads=[("acc", b), ("bt", b)], writes=[("acc", b)])
                DMA("act", yT[YC + ct * 128:YC + (ct + 1) * 128, :], acc[b][:], reads=[("acc", b)], writes=["yT"])
            ph.done()

        if want("cf"):
            ph = Ph()
            C = load_consts(ph, ["ones"])
            cw = ph.t("cfcw", [128, 4, 31]); cg = ph.t("cfg", [128, 4]); cb = ph.t("cfb", [128, 4])
            DMA("sp", cw[:], cf_cw[l], writes=["cw"])
            DMA("sp", cg[:], cf_g[l], writes=["cg"])
            DMA("sp", cb[:], cf_b[l], writes=["cb"])
            eps_t = ph.t("eps_t", [128, 1])
            MEMSET(eps_t[:], EPS, writes=["eps"])
            at = [ph.t("cfa%d" % i, [128, S]) for i in range(2)]
            gt = [ph.t("cfgt%d" % i, [128, S]) for i in range(2)]
            vp = [ph.t("cfv%d" % i, [128, S + 30]) for i in range(2)]
            conv = ph.t("cfconv", [128, 4, S])
            sq = [ph.t("cfsq%d" % i, [128, 512]) for i in range(2)]
            mean = ph.t("cfmean", [128, 512]); m2 = ph.t("cfm2", [128, 512]); rstd = ph.t("cfrstd", [128, 512])
            tmp = [ph.t("cftmp%d" % i, [128, 512]) for i in range(2)]
            for i in range(2):
                MEMSET(vp[i][:, 0:30], 0.0, writes=[("vp", i, "h")])
            for ct in range(4):
                b = ct % 2
                r0 = R_CF + ct * 128
                DMA("sp", at[b][:], projT[r0:r0 + 128, :], reads=["projT"], writes=[("at", b)])
                DMA("sp", gt[b][:], projT[r0 + 512:r0 + 640, :], reads=["projT"], writes=[("gt", b)])
                ACT(gt[b][:], gt[b][:], AF.Sigmoid, reads=[("gt", b)], writes=[("gt", b)])
                TT(vp[b][:, 30:S + 30], at[b][:], gt[b][:], ALU.mult, reads=[("at", b), ("gt", b)], writes=[("vp", b, "d")])
                TS(conv[:, ct, :], vp[b][:, 0:S], cw[:, ct, 0:1], None, ALU.mult, reads=[("vp", b), "cw"], writes=[("conv", ct)])
                for k in range(1, 31):
                    if k % 4 == 0:
                        pump(1)
                    STT(conv[:, ct, :], vp[b][:, k:k + S], cw[:, ct, k:k + 1], conv[:, ct, :], ALU.mult, ALU.add,
                        reads=[("vp", b), "cw", ("conv", ct)], writes=[("conv", ct)])
            for tc in range(4):
                sl = slice(tc * 512, (tc + 1) * 512)
                pump(1)
                for ct in range(4):
                    MM(ps[0][:, :], C["ones"][:], conv[:, ct, sl], ct == 0, ct == 3, reads=["c_ones", "conv"], writes=[("ps", 0)])
                for ct in range(4):
                    ACT(sq[ct % 2][:], conv[:, ct, sl], AF.Square, reads=["conv"], writes=[("sq", ct % 2)])
                    MM(ps[1][:, :], C["ones"][:], sq[ct % 2][:], ct == 0, ct == 3, reads=["c_ones", ("sq", ct % 2)], writes=[("ps", 1)])
                ACT(mean[:], ps[0][:, :], AF.Identity, reads=[("ps", 0)], writes=["mean"], scale=1.0 / 512)
                TT(m2[:], mean[:], mean[:], ALU.mult, reads=["mean"], writes=["m2"])
                STT(rstd[:], ps[1][:, :], 1.0 / 512, m2[:], ALU.mult, ALU.subtract, reads=[("ps", 1), "m2"], writes=["rstd"])
                ACT(rstd[:], rstd[:], AF.Sqrt, reads=["rstd", "eps"], writes=["rstd"], bias=eps_t[:, 0:1], scale=1.0)
                RECIP(rstd[:], rstd[:], reads=["rstd"], writes=["rstd"])
                for ct in range(4):
                    tb = ct % 2
                    TT(tmp[tb][:], conv[:, ct, sl], mean[:], ALU.subtract, reads=["conv", "mean"], writes=[("tmp", tb)])
                    TT(tmp[tb][:], tmp[tb][:], rstd[:], ALU.mult, reads=[("tmp", tb), "rstd"], writes=[("tmp", tb)])
                    ACT(tmp[tb][:], tmp[tb][:], AF.Silu, reads=[("tmp", tb), "cg", "cb"], writes=[("tmp", tb)],
                        scale=cg[:, ct:ct + 1], bias=cb[:, ct:ct + 1])
                    DMA("act", yT[YD + ct * 128:YD + (ct + 1) * 128, sl], tmp[tb][:], reads=[("tmp", tb)], writes=["yT"])
            ph.done()

        if want("ssd"):
            ph = Ph()
            C = load_consts(ph, ["ident", "triu", "ones"])
            ident, triu, ones = C["ident"], C["triu"], C["ones"]
            cw = ph.t("ssdcw", [128, 10, 4]); cbias = ph.t("ssdcb", [128, 10])
            DMA("sp", cw[:], ssd_cw[l], writes=["cw"]); DMA("sp", cbias[:], ssd_cb[l], writes=["cb"])
            dtb = ph.t("dtb", [128, 12]); Abc = ph.t("Abc", [128, 12]); Dbc = ph.t("Dbc", [128, 12]); ngbc = ph.t("ngbc", [128, 768])
            one_t = ph.t("one_t", [128, 1]); eps_t = ph.t("eps_t", [128, 1])
            MEMSET(one_t[:], 1.0, writes=["one"]); MEMSET(eps_t[:], EPS, writes=["eps"])
            DMA("sp", dtb[:], ssd_dtb[l:l + 1, :].partition_broadcast(128), writes=["dtb"])
            DMA("sp", Abc[:], ssd_alog[l:l + 1, :].partition_broadcast(128), writes=["Abc"])
            DMA("sp", Dbc[:], ssd_dsk[l:l + 1, :].partition_broadcast(128), writes=["Dbc"])
            DMA("sp", ngbc[:], ssd_ng[l:l + 1, :].partition_broadcast(128), writes=["ngbc"])
            ACT(Abc[:], Abc[:], AF.Exp, reads=["Abc"], writes=["Abc"])
            TS(Abc[:], Abc[:], -1.0, None, ALU.mult, reads=["Abc"], writes=["Abc"])
            xwin = [ph.t("xwin%d" % i, [128, 10, 131]) for i in range(2)]
            zt = [ph.t("zt%d" % i, [128, 768]) for i in range(2)]
            dtr = [ph.t("dtr%d" % i, [128, 12]) for i in range(2)]
            xc = ph.t("xc", [128, 10, 128])
            xtm = ph.t("xtm", [128, 768]); Btm = ph.t("Btm", [128, 256])
            dtv = ph.t("dtv", [128, 12]); av = ph.t("av", [128, 12]); ncum = ph.t("ncum", [128, 12])
            tria = ph.t("tria", [128, 12, 128])
            dl = ph.t("dl", [128, 12, 128]); ecum = ph.t("ecum", [128, 12, 128])
            GTm = ph.t("GTm", [128, 4, 128]); MT = ph.t("MT", [128, 12, 128])
            xdt = ph.t("xdt", [128, 12, 64]); xdtd = ph.t("xdtd", [128, 12, 64])
            CpT = ph.t("CpT", [128, 4, 3, 128])
            Cmask = ph.t("Cmask", [128, 4, 128])
            Sst = ph.t("Sst", [128, 2, 3, 64])
            y1 = ph.t("y1", [128, 768]); sz = ph.t("sz", [128, 768]); junk = ph.t("junk", [128, 768])
            ss = ph.t("ss", [128, 1]); rs = ph.t("rs", [128, 1])
            yTa = ph.t("yTa", [128, 6, 128])
            MEMSET(Sst[:], 0.0, writes=["Sst"])
            MEMSET(CpT[:], 0.0, writes=["CpT"])
            MEMSET(Cmask[:], 0.0, writes=["Cmask"])
            for i in range(2):
                MEMSET(xwin[i][:, :, 0:3], 0.0, writes=[("xwin", i)])
            for c in range(NT):
                b = c % 2
                t0 = c * 128
                if c == 0:
                    DMA("sp", xwin[b][:, :, 3:131], projT[0:1280, 0:128].rearrange("(t p) l -> p t l", p=128),
                        reads=["projT"], writes=[("xwin", b)])
                else:
                    DMA("sp", xwin[b][:, :, :], projT[0:1280, t0 - 3:t0 + 128].rearrange("(t p) l -> p t l", p=128),
                        reads=["projT"], writes=[("xwin", b)])
                DMA("sp", zt[b][:], z_tm[t0:t0 + 128, :], reads=["z_tm"], writes=[("zt", b)])
                DMA("sp", dtr[b][:], dt_tm[t0:t0 + 128, :], reads=["dt_tm"], writes=[("dtr", b)])
                TT(dtv[:], dtr[b][:], dtb[:], ALU.add, reads=[("dtr", b), "dtb"], writes=["dtv"])
                ACT(dtv[:], dtv[:], AF.Exp, reads=["dtv"], writes=["dtv"])
                ACT(dtv[:], dtv[:], AF.Ln, reads=["dtv", "one"], writes=["dtv"], bias=one_t[:, 0:1], scale=1.0)
                ACT(sz[:], zt[b][:], AF.Silu, reads=[("zt", b)], writes=["sz"])

                def conv_tiles(t_lo, t_hi):
                    for t in range(t_lo, t_hi):
                        TS(xc[:, t, :], xwin[b][:, t, 0:128], cw[:, t, 0:1], cbias[:, t:t + 1], ALU.mult, ALU.add,
                           reads=[("xwin", b), "cw", "cb"], writes=[("xc", t)])
                        for k in range(1, 4):
                            STT(xc[:, t, :], xwin[b][:, t, k:k + 128], cw[:, t, k:k + 1], xc[:, t, :], ALU.mult, ALU.add,
                                reads=[("xwin", b), "cw", ("xc", t)], writes=[("xc", t)])
                conv_tiles(0, 5)
                pump(1)
                TT(av[:], dtv[:], Abc[:], ALU.mult, reads=["dtv", "Abc"], writes=["av"])
                MM(ps[2][:, 0:12], triu[:], av[:], True, True, reads=["c_triu", "av"], writes=[("ps", 2)])
                TS(ncum[:], ps[2][:, 0:12], -1.0, None, ALU.mult, reads=[("ps", 2)], writes=["ncum"])
                for j in range(12):
                    TS(tria[:, j, :], triu[:], av[:, j:j + 1], None, ALU.mult, reads=["c_triu", "av"], writes=[("tria", j)])
                for q in range(3):
                    MM(ps[2 + q][:, :], ones[:], tria[:, q * 4:(q + 1) * 4, :].rearrange("p j l -> p (j l)"), True, True,
                       reads=["c_ones", "tria"], writes=[("ps", 2 + q)])
                conv_tiles(5, 10)
                ACT(xc[:], xc[:], AF.Silu, reads=["xc"], writes=["xc"])
                pump(1)
                for t in range(8):
                    pb = t // 4
                    TR(ps[pb][:, (t % 4) * 128:(t % 4 + 1) * 128], xc[:, t, :], ident[:], reads=["xc", "c_ident"], writes=[("ps", pb, t % 4)])
                ACT(xtm[:, 0:512], ps[0][:, :], AF.Copy, reads=[("ps", 0)], writes=["xtm"])
                ACT(xtm[:, 512:768], ps[1][:, 0:256], AF.Copy, reads=[("ps", 1)], writes=["xtm"])
                CP(Btm[:], ps[1][:, 256:512], reads=[("ps", 1)], writes=["Btm"])
                for q in range(3):
                    psv = ps[2 + q][:, :].rearrange("p (j l) -> p j l", j=4)
                    nb = ncum[:, q * 4:(q + 1) * 4].unsqueeze(2).to_broadcast([128, 4, 128])
                    TT(dl[:, q * 4:(q + 1) * 4, :], psv, nb, ALU.add, reads=[("ps", 2 + q), "ncum"], writes=[("dl", q)])
                    ACT(ecum[:, q * 4:(q + 1) * 4, :], psv, AF.Exp, reads=[("ps", 2 + q)], writes=[("ecum", q), ("ps", 2 + q)])
                TS(dl[:], dl[:], 0.0, None, ALU.min, reads=["dl"], writes=["dl"])
                ACT(dl[:], dl[:], AF.Exp, reads=["dl"], writes=["dl"])
                for g in range(4):
                    hp = (g % 2) * 64
                    CP(Cmask[hp:hp + 64, g, :], xc[hp:hp + 64, 8 + g // 2, :], reads=["xc"], writes=[("Cmask", g)])
                    MM(ps[5][:, g * 128:(g + 1) * 128], xc[:, 6 + g // 2, :], Cmask[:, g, :], True, True,
                       reads=["xc", ("Cmask", g)], writes=[("ps", 5, g)])
                TT(GTm[:], ps[5][:, :].rearrange("p (g l) -> p g l", g=4), triu[:].unsqueeze(1).to_broadcast([128, 4, 128]), ALU.mult,
                   reads=[("ps", 5), "c_triu"], writes=["GTm"])
                for g in range(4):
                    TT(MT[:, 3 * g:3 * g + 3, :], dl[:, 3 * g:3 * g + 3, :], GTm[:, g, :].unsqueeze(1).to_broadcast([128, 3, 128]), ALU.mult,
                       reads=["dl", "GTm"], writes=[("MT", g)])
                pump(1)
                xv = xtm[:].rearrange("p (j d) -> p j d", j=12)
                TT(xdt[:], xv, dtv[:].unsqueeze(2).to_broadcast([128, 12, 64]), ALU.mult, reads=["xtm", "dtv"], writes=["xdt"])
                TT(xdtd[:], xdt[:], dl[:, :, 127:128].to_broadcast([128, 12, 64]), ALU.mult, reads=["xdt", "dl"], writes=["xdtd"])
                for g in range(4):
                    hp = (g % 2) * 64
                    TT(CpT[hp:hp + 64, g, :, :], ecum[hp:hp + 64, 3 * g:3 * g + 3, :],
                       xc[hp:hp + 64, 8 + g // 2, :].unsqueeze(1).to_broadcast([64, 3, 128]), ALU.mult,
                       reads=["ecum", "xc"], writes=[("CpT", g)])
                TT(y1[:].rearrange("p (j d) -> p j d", j=12), xv, Dbc[:].unsqueeze(2).to_broadcast([128, 12, 64]), ALU.mult,
                   reads=["xtm", "Dbc"], writes=["y1"])
                for j in range(12):
                    g = j // 3
                    hp = (g % 2) * 64
                    pb, co = (3, j * 64) if j < 8 else (4, 256 + (j - 8) * 64)
                    key = ("ps", pb)
                    MM(ps[pb][:, co:co + 64], MT[:, j, :], xdt[:, j, :], True, False, reads=[("MT", g), "xdt"], writes=[key])
                    MM(ps[pb][:, co:co + 64], CpT[:, g, j % 3, :], Sst[:, g // 2, j % 3, :], False, True,
                       reads=[("CpT", g), "Sst"], writes=[key])
                pump(1)
                for m in range(2):
                    MM(ps[m][:, 0:384], Btm[:, m * 128:(m + 1) * 128], xdtd[:, 6 * m:6 * m + 6, :].rearrange("p j d -> p (j d)"), True, True,
                       reads=["Btm", "xdtd"], writes=[("ps", m)])
                pump(1)
                TT(y1[:, 0:512], y1[:, 0:512], ps[3][:, :], ALU.add, reads=["y1", ("ps", 3)], writes=["y1"])
                TT(y1[:, 512:768], y1[:, 512:768], ps[4][:, 256:512], ALU.add, reads=["y1", ("ps", 4)], writes=["y1"])
                TT(y1[:], y1[:], sz[:], ALU.mult, reads=["y1", "sz"], writes=["y1"])
                ACT(junk[:], y1[:], AF.Square, reads=["y1"], writes=["junk", "ss"], accum_out=ss[:, 0:1])
                ACT(rs[:, 0:1], ss[:, 0:1], AF.Sqrt, reads=["ss", "eps"], writes=["rs"], bias=eps_t[:, 0:1], scale=1.0 / 768)
                for m in range(2):
                    for gl in range(2):
                        g = 2 * m + gl
                        hp = gl * 64
                        eb = ecum[hp:hp + 64, 3 * g:3 * g + 3, 127:128].to_broadcast([64, 3, 64])
                        TT(Sst[hp:hp + 64, m, :, :], Sst[hp:hp + 64, m, :, :], eb, ALU.mult, reads=["Sst", "ecum"], writes=["Sst"])
                        TT(Sst[hp:hp + 64, m, :, :], Sst[hp:hp + 64, m, :, :],
                           ps[m][hp:hp + 64, gl * 192:gl * 192 + 192].rearrange("p (j d) -> p j d", j=3), ALU.add,
                           reads=["Sst", ("ps", m)], writes=["Sst"])
                RECIP(rs[:, 0:1], rs[:, 0:1], reads=["rs"], writes=["rs"])
                STT(y1[:], y1[:], rs[:, 0:1], ngbc[:], ALU.mult, ALU.mult, reads=["y1", "rs", "ngbc"], writes=["y1"])
                for t in range(6):
                    pb = 2 if t < 4 else 5
                    TR(ps[pb][:, (t % 4) * 128:(t % 4 + 1) * 128], y1[:, t * 128:(t + 1) * 128], ident[:], reads=["y1", "c_ident"], writes=[("ps", pb)])
                ACT(yTa[:, 0:4, :], ps[2][:, :].rearrange("p (t l) -> p t l", t=4), AF.Copy, reads=[("ps", 2)], writes=["yTa"])
                CP(yTa[:, 4:6, :], ps[5][:, 0:256].rearrange("p (t l) -> p t l", t=2), reads=[("ps", 5)], writes=["yTa"])
                DMA("act", yT[YA:YA + 768, t0:t0 + 128].rearrange("(t p) l -> p t l", p=128), yTa[:], reads=["yTa"], writes=["yT"])
            ph.done()

        if gph is not None:
            for _ in gen:
                pass
            gph.done()

        if want("merge"):
            ph = Ph()
            wout = ph.t("wout", [128, 8, D])
            DMA("sp", wout[:], w_out[l].rearrange("(k p) c -> p k c", p=128), writes=["wout"])
            ytc = [ph.t("ytc%d" % i, [128, 18, 512]) for i in range(2)]
            wb = [ph.t("wb%d" % i, [128, 18, 128]) for i in range(2)]
            gt = [ph.t("mg%d" % i, [128, 4, 512]) for i in range(2)]
            mT = ph.t("mT", [128, 8, 512])
            tmp = ph.t("mtmp", [128, 512])
            xo = [ph.t("mxo%d" % i, [128, D]) for i in range(2)]
            br_tiles = [(0, 6), (6, 10), (10, 14), (14, 18)]
            it = 0
            for tc in range(4):
                sl = slice(tc * 512, (tc + 1) * 512)
                yb = tc % 2
                DMA("sp", ytc[yb][:], yT[:, sl].rearrange("(t p) l -> p t l", p=128), reads=["yT"], writes=[("ytc", yb)])
                for dt_ in range(8):
                    b = it % 2
                    it += 1
                    DMA("sp", wb[b][:], w_branch[l, :, dt_ * 128:(dt_ + 1) * 128].rearrange("(t p) d -> p t d", p=128), writes=[("wb", b)])
                    DMA("sp", gt[b][:], projT[R_GATE:R_END, sl].rearrange("(b r) l -> r b l", b=4)[dt_ * 128:(dt_ + 1) * 128],
                        reads=["projT"], writes=[("gt", b)])
                    ACT(gt[b][:], gt[b][:], AF.Sigmoid, reads=[("gt", b)], writes=[("gt", b)])
                    for bi, (ta, tb_) in enumerate(br_tiles):
                        pb = 4 + bi
                        for t in range(ta, tb_):
                            MM(ps[pb][:, :], wb[b][:, t, :], ytc[yb][:, t, :], t == ta, t == tb_ - 1,
                               reads=[("wb", b), ("ytc", yb)], writes=[("ps", pb)])
                        if bi == 0:
                            TT(mT[:, dt_, :], ps[pb][:, :], gt[b][:, bi, :], ALU.mult, reads=[("ps", pb), ("gt", b)], writes=[("mT", dt_)])
                        else:
                            TT(tmp[:], ps[pb][:, :], gt[b][:, bi, :], ALU.mult, reads=[("ps", pb), ("gt", b)], writes=["tmp"])
                            TT(mT[:, dt_, :], mT[:, dt_, :], tmp[:], ALU.add, reads=[("mT", dt_), "tmp"], writes=[("mT", dt_)])
                for t4 in range(4):
                    tt = tc * 4 + t4
                    xb = tt % 2
                    DMA("sp", xo[xb][:], xsrc[tt * 128:(tt + 1) * 128, :], reads=["xres"], writes=[("xo", xb)])
                    for half in range(2):
                        pb = half
                        for k in range(8):
                            MM(ps[pb][:, :], mT[:, k, t4 * 128:(t4 + 1) * 128], wout[:, k, half * 512:(half + 1) * 512], k == 0, k == 7,
                               reads=["mT", "wout"], writes=[("ps", pb)])
                        TT(xo[xb][:, half * 512:(half + 1) * 512], xo[xb][:, half * 512:(half + 1) * 512], ps[pb][:, :], ALU.add,
                           reads=[("xo", xb), ("ps", pb)], writes=[("xo", xb)])
                    DMA("pool", xres[tt * 128:(tt + 1) * 128, :], xo[xb][:], reads=[("xo", xb)], writes=[("xres", tt)])
            ph.done()

        if want("peer"):
            ph = Ph()
            C = load_consts(ph, ["ident", "iota16"])
            ident, iota16 = C["ident"], C["iota16"]
            wqt = ph.t("wqt", [128, 8, 2048])
            DMA("sp", wqt[:], wq[l].rearrange("(k p) c -> p k c", p=128), writes=["wqt"])
            skt = ph.t("skt", [128, 2, 128])
            DMA("sp", skt[:], skT[l].rearrange("two d k -> d two k"), writes=["skt"])
            gbc = ph.t("gbc", [128, D])
            DMA("sp", gbc[:], g2[l:l + 1, :].partition_broadcast(128), writes=["gbc"])
            eps_t = ph.t("eps_t", [128, 1])
            MEMSET(eps_t[:], EPS, writes=["eps"])
            junk = ph.t("junk", [128, D]); junk2 = ph.t("junk2", [128, D]); ss = ph.t("ss", [128, 1]); rs = ph.t("rs", [128, 1])
            xt_ = [ph.t("pxt%d" % i, [128, D]) for i in range(2)]; h2_ = [ph.t("ph2%d" % i, [128, D]) for i in range(2)]; h2T = ph.t("h2T", [128, 8, 128])
            qT = ph.t("qT", [128, 16, 128]); sc = ph.t("psc", [128, 16, 128]); wk4 = [ph.t("pwk%d" % i, [128, 256]) for i in range(4)]
            mx = ph.t("pmx", [128, 16, 16]); mi = ph.t("pmi", [128, 16, 16], U16); mif = ph.t("pmif", [128, 16, 16])
            cand = ph.t("cand", [128, 8, 256])
            best = ph.t("best", [128, 8, 16]); pos = ph.t("ppos", [128, 8, 16], U16); posf = ph.t("posf", [128, 8, 16])
            ai = ph.t("pai", [128, 128], I32); af = ph.t("paf", [128, 128]); bf = ph.t("pbf", [128, 128])
            eq = ph.t("peq", [128, 8, 16, 16])
            i0s = ph.t("i0s", [128, 128]); i1s = ph.t("i1s", [128, 128]); idxf = ph.t("idxf", [128, 128])
            gate_ = [ph.t("gate%d" % i, [128, 8, 16]) for i in range(2)]; gs = ph.t("gs", [128, 8])
            idxi_ = [ph.t("idxi%d" % i, [128, 128], I32) for i in range(2)]
            AT = ph.t("AT", [128, 128]); actT = ph.t("actT", [128, 128])
            NB = 16
            ur = [ph.t("ur%d" % i, [128, D]) for i in range(NB)]
            def routeA(tt, bb):
                xt, h2, gate, idxi = xt_[bb], h2_[bb], gate_[bb], idxi_[bb]
                KX, KH, KG, KI = ("xt", bb), ("h2", bb), ("gate", bb), ("idxi", bb)
                t0 = tt * 128
                DMA("sp", xt[:], xres[t0:t0 + 128, :], reads=[("xres", tt)], writes=[KX])
                rmsnorm_rows(ph, xt[:], gbc[:], h2[:], junk[:], ss, rs, eps_t, D, KX, KH)
                for half in range(2):
                    for k in range(4):
                        kk = half * 4 + k
                        TR(ps[half][:, k * 128:(k + 1) * 128], h2[:, kk * 128:(kk + 1) * 128], ident[:], reads=[KH, "c_ident"], writes=[("ps", half)])
                    ACT(h2T[:, half * 4:(half + 1) * 4, :], ps[half][:, :].rearrange("p (k t) -> p k t", k=4), AF.Copy,
                        reads=[("ps", half)], writes=["h2T"])
                for hq in range(4):
                    pb = 2 + hq % 2
                    for j in range(4):
                        hp = hq * 4 + j
                        for k in range(8):
                            MM(ps[pb][:, j * 128:(j + 1) * 128], wqt[:, k, hp * 128:(hp + 1) * 128], h2T[:, k, :], k == 0, k == 7,
                               reads=["wqt", "h2T"], writes=[("ps", pb, j)])
                    ACT(qT[:, hq * 4:(hq + 1) * 4, :], ps[pb][:, :].rearrange("p (j t) -> p j t", j=4), AF.Copy, reads=[("ps", pb)], writes=[("qT", hq)])
                for hq in range(4):
                    pb = 4 + hq % 2
                    for j in range(4):
                        hp = hq * 4 + j
                        MM(ps[pb][:, j * 128:(j + 1) * 128], qT[:, hp, :], skt[:, hp % 2, :], True, True,
                           reads=[("qT", hq), "skt"], writes=[("ps", pb, j)])
                    ACT(sc[:, hq * 4:(hq + 1) * 4, :], ps[pb][:, :].rearrange("p (j t) -> p j t", j=4), AF.Copy, reads=[("ps", pb)], writes=[("sc", hq)])

            def routeA2(tt, bb, grp=None, part=None):
                xt, h2, gate, idxi = xt_[bb], h2_[bb], gate_[bb], idxi_[bb]
                KX, KH, KG, KI = ("xt", bb), ("h2", bb), ("gate", bb), ("idxi", bb)
                if part is None:
                    for hp0 in (range(0, 16, 4) if grp is None else [4 * grp]):
                        hps = list(range(hp0, hp0 + 4))
                        for hp in hps:
                            P.op("dve", lambda e, hp=hp: e.max(out=mx[:, hp, 0:8], in_=sc[:, hp, :]), reads=[("sc", hp // 4)], writes=[("mx", hp, 0)])
                        for hp in hps:
                            P.op("dve", lambda e, hp=hp: e.max_index(out=mi[:, hp, 0:8], in_max=mx[:, hp, 0:8], in_values=sc[:, hp, :]),
                                 reads=[("sc", hp // 4), ("mx", hp, 0)], writes=[("mi", hp, 0)])
                        for hp in hps:
                            P.op("dve", lambda e, hp=hp: e.match_replace(out=wk4[hp % 4][:, 0:128], in_to_replace=mx[:, hp, 0:8], in_values=sc[:, hp, :], imm_value=-1e30),
                                 reads=[("sc", hp // 4), ("mx", hp, 0)], writes=[("wk", hp % 4)])
                        for hp in hps:
                            P.op("dve", lambda e, hp=hp: e.max(out=mx[:, hp, 8:16], in_=wk4[hp % 4][:, 0:128]), reads=[("wk", hp % 4)], writes=[("mx", hp, 1)])
                        for hp in hps:
                            P.op("dve", lambda e, hp=hp: e.max_index(out=mi[:, hp, 8:16], in_max=mx[:, hp, 8:16], in_values=wk4[hp % 4][:, 0:128]),
                                 reads=[("wk", hp % 4), ("mx", hp, 1)], writes=[("mi", hp, 1)])
                if part == 1 or (part is None and grp is None):
                    CP(mif[:], mi[:], reads=["mi"], writes=["mif"])
                    mxv = mx[:].rearrange("p (h two) k -> p h two k", two=2)
                    TT(cand[:].rearrange("p h (a b) -> p h a b", a=16),
                       mxv[:, :, 0, :].unsqueeze(3).to_broadcast([128, 8, 16, 16]),
                       mxv[:, :, 1, :].unsqueeze(2).to_broadcast([128, 8, 16, 16]), ALU.add, reads=["mx"], writes=["cand"])
                    for h0 in range(0, 8, 4):
                        hs = list(range(h0, h0 + 4))
                        for h in hs:
                            P.op("dve", lambda e, h=h: e.max(out=best[:, h, 0:8], in_=cand[:, h, :]), reads=["cand"], writes=[("best", h, 0)])
                        for h in hs:
                            P.op("dve", lambda e, h=h: e.max_index(out=pos[:, h, 0:8], in_max=best[:, h, 0:8], in_values=cand[:, h, :]),
                                 reads=["cand", ("best", h, 0)], writes=[("pos", h, 0)])
                        for h in hs:
                            P.op("dve", lambda e, h=h: e.match_replace(out=wk4[h % 4][:], in_to_replace=best[:, h, 0:8], in_values=cand[:, h, :], imm_value=-1e30),
                                 reads=["cand", ("best", h, 0)], writes=[("wk", h % 4)])
                        for h in hs:
                            P.op("dve", lambda e, h=h: e.max(out=best[:, h, 8:16], in_=wk4[h % 4][:]), reads=[("wk", h % 4)], writes=[("best", h, 1)])
                        for h in hs:
                            P.op("dve", lambda e, h=h: e.max_index(out=pos[:, h, 8:16], in_max=best[:, h, 8:16], in_values=wk4[h % 4][:]),
                                 reads=[("wk", h % 4), ("best", h, 1)], writes=[("pos", h, 1)])
                if part == 2 or (part is None and grp is None):
                    CP(posf[:], pos[:], reads=["pos"], writes=["posf"])
                    pfl = posf[:].rearrange("p h k -> p (h k)")
                    TS(ai[:], pfl, 1.0 / 16, -0.46875, ALU.mult, ALU.add, reads=["posf"], writes=["ai"])
                    CP(af[:], ai[:], reads=["ai"], writes=["af"])
                    STT(bf[:], af[:], -16.0, pfl, ALU.mult, ALU.add, reads=["af", "posf"], writes=["bf"])
                    mfv = mif[:].rearrange("p (h two) k -> p h two k", two=2)
                    io_b = iota16[:].unsqueeze(1).unsqueeze(1).to_broadcast([128, 8, 16, 16])
                    for (sel, src, which) in ((i0s, af, 0), (i1s, bf, 1)):
                        sv = src[:].rearrange("p (h k) -> p h k", h=8).unsqueeze(3).to_broadcast([128, 8, 16, 16])
                        TT(eq[:], sv, io_b, ALU.is_equal, reads=["af", "bf", "c_iota16"], writes=["eq"])
                        TT(eq[:], eq[:], mfv[:, :, which, :].unsqueeze(2).to_broadcast([128, 8, 16, 16]), ALU.mult, reads=["eq", "mif"], writes=["eq"])
                        P.op("dve", lambda e, sel=sel: e.reduce_sum(out=sel[:], in_=eq[:].rearrange("p h k a -> p (h k) a"), axis=AX.X),
                             reads=["eq"], writes=["i0s" if which == 0 else "i1s"])
                    STT(idxf[:], i0s[:], 128.0, i1s[:], ALU.mult, ALU.add, reads=["i0s", "i1s"], writes=["idxf"])
                    TS(idxf[:], idxf[:], 0.0, 16383.0, ALU.max, ALU.min, reads=["idxf"], writes=["idxf"])
                    TT(gate[:], best[:], best[:, :, 0:1].to_broadcast([128, 8, 16]), ALU.subtract, reads=["best"], writes=[KG])
                    ACT(gate[:], gate[:], AF.Exp, reads=[KG], writes=[KG])
                    P.op("dve", lambda e: e.reduce_sum(out=gs[:], in_=gate[:], axis=AX.X), reads=[KG], writes=["gs"])
                    RECIP(gs[:], gs[:], reads=["gs"], writes=["gs"])
                    TT(gate[:], gate[:], gs[:].unsqueeze(2).to_broadcast([128, 8, 16]), ALU.mult, reads=[KG, "gs"], writes=[KG])

                    CP(idxi[:], idxf[:], reads=["idxf"], writes=[KI])

            def expertB(tt, bb):
                xt, h2, gate, idxi = xt_[bb], h2_[bb], gate_[bb], idxi_[bb]
                KX, KH, KG, KI = ("xt", bb), ("h2", bb), ("gate", bb), ("idxi", bb)
                t0 = tt * 128
                gfl = gate[:].rearrange("p h k -> p (h k)")
                for sl_ in range(128):
                    b = sl_ % NB
                    P.dma("pool", lambda e, b=b, sl_=sl_: e.indirect_dma_start(
                        out=ur[b][:], out_offset=None, in_=peer_u[l],
                        in_offset=bass.IndirectOffsetOnAxis(ap=idxi[:, sl_:sl_ + 1], axis=0)),
                        reads=[KI], writes=[("ur", b)])
                    STT(junk2[:], ur[b][:], 1.0, h2[:], ALU.mult, ALU.mult, reads=[("ur", b), KH], writes=["junk2", ("AT", sl_)],
                        accum_out=AT[:, sl_:sl_ + 1])
                    if tt + 1 < NT and sl_ in (39, 71, 103):
                        routeA2(tt + 1, (tt + 1) % 2, grp=(sl_ - 39) // 32)
                if tt + 1 < NT:
                    routeA2(tt + 1, (tt + 1) % 2, grp=3)
                ACT(actT[:], AT[:], AF.Gelu_apprx_tanh, reads=["AT"], writes=["actT"])
                TT(actT[:], actT[:], gfl, ALU.mult, reads=["actT", KG], writes=["actT"])
                for sl_ in range(128):
                    b = sl_ % NB
                    P.dma("pool", lambda e, b=b, sl_=sl_: e.indirect_dma_start(
                        out=ur[b][:], out_offset=None, in_=peer_v[l],
                        in_offset=bass.IndirectOffsetOnAxis(ap=idxi[:, sl_:sl_ + 1], axis=0)),
                        reads=[KI], writes=[("ur", b)])
                    STT(xt[:], ur[b][:], actT[:, sl_:sl_ + 1], xt[:], ALU.mult, ALU.add, reads=[("ur", b), "actT", KX], writes=[KX])
                    if tt + 1 < NT and sl_ in (23, 63):
                        routeA2(tt + 1, (tt + 1) % 2, part=(1 if sl_ == 23 else 2))
                DMA("sp", xres[t0:t0 + 128, :], xt[:], reads=[KX], writes=[("xres", tt)])

            routeA(0, 0)
            routeA2(0, 0)
            for tt in range(NT):
                if tt + 1 < NT:
                    routeA(tt + 1, (tt + 1) % 2)
                expertB(tt, tt % 2)
            ph.done()

    if want("final"):
        ph = Ph()
        gbc = ph.t("gbc", [128, D]); eps_t = ph.t("eps_t", [128, 1])
        junk = ph.t("junk", [128, D]); ss = ph.t("ss", [128, 1]); rs = ph.t("rs", [128, 1])
        xt = [ph.t("fxt%d" % i, [128, D]) for i in range(2)]
        ht = [ph.t("fht%d" % i, [128, D]) for i in range(2)]
        MEMSET(eps_t[:], EPS, writes=["eps"])
        DMA("sp", gbc[:], gfin[0:1, :].partition_broadcast(128), writes=["gbc"])
        for tt in range(NT):
            b = tt % 2
            DMA("sp", xt[b][:], xres[tt * 128:(tt + 1) * 128, :], reads=["xres"], writes=[("xt", b)])
            rmsnorm_rows(ph, xt[b][:], gbc[:], ht[b][:], junk[:], ss, rs, eps_t, D, ("xt", b), ("ht", b))
            DMA("pool", y_out[tt * 128:(tt + 1) * 128, :], ht[b][:], reads=[("ht", b)], writes=["y"])
        ph.done()
    P.close()
    return nc


def prep_shared(inp):
    f = lambda a: np.ascontiguousarray(np.asarray(a, dtype=np.float32))
    sh = {}
    for k in ("norm1_g", "norm2_g", "w_in", "ssd_dt_bias", "ssd_a_log", "ssd_d", "ssd_norm_g", "s5_w_glu",
              "w_branch", "w_out", "peer_w_query"):
        sh[k] = f(inp[k])
    for i in range(L):
        sh["peer_u%d" % i] = f(np.asarray(inp["peer_u"])[i])
        sh["peer_v%d" % i] = f(np.asarray(inp["peer_v"])[i])
    sh["final_norm_g"] = f(inp["final_norm_g"]).reshape(1, D)
    sh["ssd_cw"] = f(np.transpose(np.asarray(inp["ssd_conv_w"]).reshape(L, 4, 10, 128), (0, 3, 2, 1)))
    sh["ssd_cb"] = f(np.transpose(np.asarray(inp["ssd_conv_b"]).reshape(L, 10, 128), (0, 2, 1)))
    sh["sc_cw"] = f(np.transpose(np.asarray(inp["sc_conv_w"]).reshape(L, 3, 4, 128), (0, 3, 2, 1)))
    sh["cf_cw"] = f(np.transpose(np.asarray(inp["cf_conv_w"]).reshape(L, 31, 4, 128), (0, 3, 2, 1)))
    sh["cf_g"] = f(np.transpose(np.asarray(inp["cf_ln_g"]).reshape(L, 4, 128), (0, 2, 1)))
    sh["cf_b"] = f(np.transpose(np.asarray(inp["cf_ln_b"]).reshape(L, 4, 128), (0, 2, 1)))
    sh["s5_dsk"] = f(np.transpose(np.asarray(inp["s5_d"]).reshape(L, 4, 128), (0, 2, 1)))
    for nm, src in (("s5_lre", "s5_lam_re"), ("s5_lim", "s5_lam_im")):
        sh[nm] = f(np.transpose(np.asarray(inp[src]).reshape(L, 16, 128), (0, 2, 1)))
    ls = np.repeat(np.asarray(inp["s5_log_step"])[:, :, None], 64, axis=2)
    sh["s5_lst"] = f(np.transpose(ls.reshape(L, 16, 128), (0, 2, 1)))
    Bre = np.zeros((L, 128, 16, 128), np.float32); Bim = np.zeros_like(Bre)
    Cre = np.zeros((L, 128, 16, 128), np.float32); Cim = np.zeros_like(Cre)
    b_re = np.asarray(inp["s5_b_re"]); b_im = np.asarray(inp["s5_b_im"])
    c_re = np.asarray(inp["s5_c_re"]); c_im = np.asarray(inp["s5_c_im"])
    for g in range(32):
        tp, gl = g // 2, g % 2
        r0 = (g % 8) * 16
        for dst, src in ((Bre, b_re), (Bim, b_im)):
            dst[:, r0:r0 + 16, tp, gl * 64:(gl + 1) * 64] = np.transpose(src[:, g], (0, 2, 1))
        for dst, src in ((Cre, c_re), (Cim, c_im)):
            dst[:, gl * 64:(gl + 1) * 64, tp, r0:r0 + 16] = np.transpose(src[:, g], (0, 2, 1))
    sh["s5_Bre"], sh["s5_Bim"], sh["s5_Cre"], sh["s5_Cim"] = Bre, Bim, Cre, Cim
    sh["skT"] = f(np.transpose(np.asarray(inp["peer_sub_keys"]), (0, 1, 3, 2)))
    sh["c_ident"] = np.eye(128, dtype=np.float32)
    sh["c_triu"] = np.triu(np.ones((128, 128), np.float32))
    sh["c_ones"] = np.ones((128, 128), np.float32)
    sh["c_iota_t"] = np.ascontiguousarray(np.broadcast_to(np.arange(S, dtype=np.float32)[None, :], (128, S)))
    sh["c_iota16"] = np.ascontiguousarray(np.broadcast_to(np.arange(16, dtype=np.float32)[None, :], (128, 16)))
    return sh


def kernel(**inputs):
    sh = prep_shared(inputs)
    x = np.asarray(inputs["x"], dtype=np.float32)
    nc = build()
    in_maps = []
    for c in range(8):
        m = dict(sh)
        m["x"] = np.ascontiguousarray(x[c])
        in_maps.append(m)
    res = run_bass_kernel_spmd(nc, in_maps, core_ids=list(range(8)))
    return np.stack([np.asarray(res.results[c]["y"]).reshape(S, D) for c in range(8)], axis=0).astype(np.float32)
```
